# Optimizing a Trainium2 kernel written in Bass

```python
import math
import jax, jax.numpy as jnp
from jax import lax
import numpy as np

D_MODEL = 4096
BATCH = 4
SEQ = 2048
DEPTH = 2
DEC_BATCH = 8
DEC_SEQ = 1
PAST_LEN = 16384
PAGE_SIZE = 128

N_A_LAYERS = DEPTH // 2
N_B_LAYERS = DEPTH - N_A_LAYERS

SSM_INNER = 3 * D_MODEL // 4
SSM_HEAD_DIM = 64
SSM_HEADS = SSM_INNER // SSM_HEAD_DIM
SSM_GROUPS = 8
SSM_HPG = SSM_HEADS // SSM_GROUPS
SSM_STATE = 128
SSM_CONV = 4
SSM_CHUNK = 128
SSM_XBC = SSM_INNER + 2 * SSM_GROUPS * SSM_STATE

DIL_PATTERNS = ((128, 1), (512, 4), (2048, 16))
N_DIL = len(DIL_PATTERNS)
DIL_HEADS = 8
HEAD_DIM = 128
ROT_DIM = HEAD_DIM // 4
ROPE_THETA = 500000.0
DIL_Q = N_DIL * DIL_HEADS * HEAD_DIM
DIL_OUT = DIL_HEADS * HEAD_DIM
KV_COLS = N_DIL * 2 * DIL_HEADS * HEAD_DIM
Q_BLOCK = 128

N_MEM = 256
MEM_HEADS = 4
MEM_HEAD_DIM = D_MODEL // 16
MEM_W = MEM_HEADS * MEM_HEAD_DIM

D_FF = 256 * ((8 * D_MODEL // 3 + 255) // 256)
FFN_CONV = 3

EPS = 1e-6
A_IN = SSM_INNER + SSM_XBC + SSM_HEADS + MEM_W
A_SPLITS = [SSM_INNER, SSM_INNER + SSM_XBC, SSM_INNER + SSM_XBC + SSM_HEADS]
B_IN = DIL_Q + MEM_W
A_MIX = SSM_INNER + MEM_W
B_MIX = DIL_OUT + MEM_W

kernel_name = 'yoco_mamba2_dilated_window_hybrid_step'


def rmsnorm(x, g):
    xf = x.astype(jnp.float32)
    y = xf * lax.rsqrt(jnp.mean(xf * xf, axis=-1, keepdims=True) + EPS)
    return (y * g.astype(jnp.float32)).astype(x.dtype)


def rope_partial(x, pos):
    half = ROT_DIM // 2
    inv_freq = jnp.exp(-(2.0 * jnp.arange(half, dtype=jnp.float32) / ROT_DIM) * math.log(ROPE_THETA))
    ang = pos.astype(jnp.float32)[:, None] * inv_freq[None, :]
    shape = (1, pos.shape[0]) + (1,) * (x.ndim - 3) + (half,)
    cos = jnp.cos(ang).reshape(shape)
    sin = jnp.sin(ang).reshape(shape)
    xr = x[..., :ROT_DIM].astype(jnp.float32)
    x1, x2 = xr[..., :half], xr[..., half:]
    rot = jnp.concatenate([x1 * cos - x2 * sin, x2 * cos + x1 * sin], axis=-1).astype(x.dtype)
    return jnp.concatenate([rot, x[..., ROT_DIM:]], axis=-1)


def causal_dwconv(x, prev, w, b):
    width = w.shape[0]
    t_len = x.shape[1]
    xp = jnp.concatenate([prev.astype(x.dtype), x], axis=1)
    y = b + sum(xp[:, k:k + t_len] * w[k] for k in range(width))
    return y, xp[:, xp.shape[1] - (width - 1):]


def ssd_scan(xdt_in, dt, a_neg, bm, cm, h0):
    x = xdt_in
    bsz, t_len = x.shape[:2]
    c = min(SSM_CHUNK, t_len)
    n_c = -(-t_len // c)
    pad = n_c * c - t_len

    def chunks(u):
        u = jnp.pad(u, ((0, 0), (0, pad)) + ((0, 0),) * (u.ndim - 2))
        return jnp.moveaxis(u.reshape((bsz, n_c, c) + u.shape[2:]), 1, 0)

    xdt = x * dt[..., None]
    log_a = dt * a_neg
    tri = jnp.tril(jnp.ones((c, c), dtype=bool))

    def step(h, inp):
        xdt_c, la_c, b_c, c_c = inp
        cum = jnp.cumsum(la_c, axis=1)
        seg = cum[:, :, None] - cum[:, None, :]
        decay = jnp.exp(jnp.where(tri[None, :, :, None, None], seg, -jnp.inf))
        cb = jnp.einsum('btgn,bsgn->btsg', c_c, b_c)
        y = jnp.einsum('btsgh,bsghp->btghp', cb[..., None] * decay, xdt_c)
        y = y + jnp.einsum('btgn,bghpn->btghp', c_c, h) * jnp.exp(cum)[..., None]
        w_end = jnp.exp(cum[:, -1:] - cum)
        h = h * jnp.exp(cum[:, -1])[..., None, None] + jnp.einsum('bsgh,bsghp,bsgn->bghpn', w_end, xdt_c, b_c)
        return h, y

    h_last, ys = lax.scan(step, h0, (chunks(xdt), chunks(log_a), chunks(bm), chunks(cm)))
    y = jnp.moveaxis(ys, 0, 1).reshape((bsz, n_c * c) + x.shape[2:])[:, :t_len]
    return y, h_last


def mamba2_mixer(z, xbc, dt_raw, h_prev, conv_prev, w_conv, b_conv, dt_bias, a_log, d_skip, g_out):
    bsz, t_len, _ = z.shape
    xbc, conv_new = causal_dwconv(xbc, conv_prev, w_conv, b_conv)
    xbc = jax.nn.silu(xbc.astype(jnp.float32))
    xs, bm, cm = jnp.split(xbc, [SSM_INNER, SSM_INNER + SSM_GROUPS * SSM_STATE], axis=-1)
    xs = xs.reshape(bsz, t_len, SSM_GROUPS, SSM_HPG, SSM_HEAD_DIM)
    bm = bm.reshape(bsz, t_len, SSM_GROUPS, SSM_STATE)
    cm = cm.reshape(bsz, t_len, SSM_GROUPS, SSM_STATE)
    dt = jax.nn.softplus(dt_raw.astype(jnp.float32) + dt_bias.astype(jnp.float32))
    dt = dt.reshape(bsz, t_len, SSM_GROUPS, SSM_HPG)
    a_neg = -jnp.exp(a_log.astype(jnp.float32)).reshape(SSM_GROUPS, SSM_HPG)
    h0 = h_prev.astype(jnp.float32).reshape(bsz, SSM_GROUPS, SSM_HPG, SSM_HEAD_DIM, SSM_STATE)
    y, h_last = ssd_scan(xs, dt, a_neg, bm, cm, h0)
    y = y + d_skip.astype(jnp.float32).reshape(SSM_GROUPS, SSM_HPG)[..., None] * xs
    u = (y.reshape(bsz, t_len, SSM_INNER) * jax.nn.silu(z.astype(jnp.float32)))
    u = u.reshape(bsz, t_len, SSM_GROUPS, SSM_INNER // SSM_GROUPS)
    u = u * lax.rsqrt(jnp.mean(u * u, axis=-1, keepdims=True) + EPS)
    u = u.reshape(bsz, t_len, SSM_INNER) * g_out.astype(jnp.float32)
    h_out = h_last.reshape(bsz, SSM_HEADS, SSM_HEAD_DIM, SSM_STATE).astype(h_prev.dtype)
    return u.astype(z.dtype), h_out, conv_new


def memory_kv(mem, g_norm, w_kv, g_k):
    bsz, n, _ = mem.shape
    kv = (rmsnorm(mem, g_norm) @ w_kv).reshape(bsz, n, 2, MEM_HEADS, MEM_HEAD_DIM)
    k = rmsnorm(kv[:, :, 0], g_k)
    return jnp.stack([k, kv[:, :, 1]], axis=2)


def memory_attention(q, mem_kv):
    s = jnp.einsum('bthd,bmhd->bhtm', q.astype(jnp.float32), mem_kv[:, :, 0].astype(jnp.float32)) * (MEM_HEAD_DIM ** -0.5)
    p = jax.nn.softmax(s, axis=-1)
    o = jnp.einsum('bhtm,bmhd->bthd', p, mem_kv[:, :, 1].astype(jnp.float32))
    return o.astype(q.dtype)


def shared_window_kv(x, pos, g_norm, w_kv, g_k):
    bsz, t_len, _ = x.shape
    kv = (rmsnorm(x, g_norm) @ w_kv).reshape(bsz, t_len, N_DIL, 2, DIL_HEADS, HEAD_DIM)
    k = rope_partial(rmsnorm(kv[:, :, :, 0], g_k[:, None, :]), pos)
    return jnp.stack([k, kv[:, :, :, 1]], axis=3)


def dilated_attention(q, kv_full, past_lens):
    bsz, t_len = q.shape[:2]
    bq = min(Q_BLOCK, t_len)
    n_blk = -(-t_len // bq)
    pad = n_blk * bq - t_len
    qp = jnp.pad(q, ((0, 0), (0, pad), (0, 0), (0, 0), (0, 0)))
    kvp = [jnp.pad(kv, ((0, 0), (w, pad), (0, 0), (0, 0), (0, 0))) for kv, (w, _) in zip(kv_full, DIL_PATTERNS)]
    scale = HEAD_DIM ** -0.5
    i_idx = np.arange(bq)[:, None]

    def block(q0):
        qb = lax.dynamic_slice_in_dim(qp, q0, bq, axis=1).astype(jnp.float32)
        outs, lses = [], []
        for g, (w, r) in enumerate(DIL_PATTERNS):
            n_k = w // r + 1
            k_idx = np.arange(n_k)[None, :]
            sl = lax.dynamic_slice_in_dim(kvp[g], q0 + past_lens[g], bq + w, axis=1)
            kvg = sl[:, i_idx + w - k_idx * r].astype(jnp.float32)
            valid = (q0 + past_lens[g] + i_idx - k_idx * r) >= 0
            s = jnp.einsum('bqhd,bqkhd->bhqk', qb[:, :, g], kvg[:, :, :, 0]) * scale
            s = jnp.where(valid[None, None], s, -jnp.inf)
            m = jnp.max(s, axis=-1, keepdims=True)
            p = jnp.exp(s - m)
            den = jnp.sum(p, axis=-1, keepdims=True)
            o = jnp.einsum('bhqk,bqkhd->bqhd', p / den, kvg[:, :, :, 1])
            outs.append(o)
            lses.append((m + jnp.log(den))[..., 0])
        wts = jax.nn.softmax(jnp.stack(lses, axis=0), axis=0)
        wts = jnp.swapaxes(wts, 2, 3)[..., None]
        return jnp.sum(wts * jnp.stack(outs, axis=0), axis=0).astype(q.dtype)

    ob = lax.map(block, jnp.arange(n_blk, dtype=jnp.int32) * bq)
    o = jnp.moveaxis(ob, 0, 1).reshape(bsz, n_blk * bq, DIL_HEADS, HEAD_DIM)
    return o[:, :t_len]


def conv_ffn(x, prev, g, w_up, w_conv, b_conv, w_down):
    gate, up = jnp.split(rmsnorm(x, g) @ w_up, [D_FF], axis=-1)
    gate, new_prev = causal_dwconv(gate, prev, w_conv, b_conv)
    return (jax.nn.silu(gate) * up) @ w_down, new_prev


def trunk(x, pos, mem_kv, ssm_prev, ssm_conv_prev, ffn_prev, win_past, p):
    bsz, t_len, _ = x.shape
    ssm_out, ssm_conv_out, ffn_out = [], [], []
    kv_new, kv_full = None, None
    past_lens = [wp.shape[1] for wp in win_past]
    for i in range(DEPTH):
        h = rmsnorm(x, p['g_mix'][i])
        if i < N_A_LAYERS:
            a = i
            z, xbc, dt_raw, q_mem = jnp.split(h @ p['w_in_a'][a], A_SPLITS, axis=-1)
            y_mix, s_new, c_new = mamba2_mixer(z, xbc, dt_raw, ssm_prev[a], ssm_conv_prev[a], p['w_conv_a'][a], p['b_conv_a'][a], p['dt_bias_a'][a], p['a_log_a'][a], p['d_skip_a'][a], p['g_ssm_out_a'][a])
            ssm_out.append(s_new)
            ssm_conv_out.append(c_new)
            w_out = p['w_out_a'][a]
        else:
            b = i - N_A_LAYERS
            if i == N_A_LAYERS:
                kv_new = shared_window_kv(x, pos, p['g_kv'], p['w_kv'], p['g_k_dil'])
                kv_full = [jnp.concatenate([win_past[g].astype(x.dtype), kv_new[:, :, g]], axis=1) for g in range(N_DIL)]
            q_dil, q_mem = jnp.split(h @ p['w_in_b'][b], [DIL_Q], axis=-1)
            q = rmsnorm(q_dil.reshape(bsz, t_len, N_DIL, DIL_HEADS, HEAD_DIM), p['g_q_dil'][b][:, None, :])
            q = rope_partial(q, pos)
            y_mix = dilated_attention(q, kv_full, past_lens).reshape(bsz, t_len, DIL_OUT)
            w_out = p['w_out_b'][b]
        qm = rmsnorm(q_mem.reshape(bsz, t_len, MEM_HEADS, MEM_HEAD_DIM), p['g_mem_q'][i])
        y_mem = memory_attention(qm, mem_kv[i]).reshape(bsz, t_len, MEM_W)
        x = x + jnp.concatenate([y_mix.astype(x.dtype), y_mem.astype(x.dtype)], axis=-1) @ w_out
        f, f_new = conv_ffn(x, ffn_prev[i], p['g_ffn'][i], p['w_ffn_up'][i], p['w_ffn_conv'][i], p['b_ffn_conv'][i], p['w_ffn_down'][i])
        ffn_out.append(f_new)
        x = x + f
    return x, jnp.stack(ssm_out, axis=0), jnp.stack(ssm_conv_out, axis=0), jnp.stack(ffn_out, axis=0), kv_new


def setup_inputs(seed: int = 0) -> dict:
    key = jax.random.key(seed)
    ks = iter(jax.random.split(key, 48))
    f32 = jnp.float32

    def nrm(shape, scale=1.0):
        return jax.random.normal(next(ks), shape, f32) * scale

    def gain(shape):
        return 1.0 + nrm(shape, 0.02)

    n_a, n_b = N_A_LAYERS, N_B_LAYERS
    l_win = [min(w, PAST_LEN) for w, _ in DIL_PATTERNS]
    inp = {}
    inp['x_prompt'] = nrm((BATCH, SEQ, D_MODEL))
    inp['x_sample'] = nrm((DEC_BATCH, DEC_SEQ, D_MODEL))
    inp['state_ssm'] = nrm((n_a, DEC_BATCH, SSM_HEADS, SSM_HEAD_DIM, SSM_STATE), 0.5)
    inp['state_ssm_conv'] = nrm((n_a, DEC_BATCH, SSM_CONV - 1, SSM_XBC))
    inp['state_ffn_conv'] = nrm((DEPTH, DEC_BATCH, FFN_CONV - 1, D_FF))
    inp['cache_mem_kv'] = nrm((DEPTH, DEC_BATCH, N_MEM, 2, MEM_HEADS, MEM_HEAD_DIM))
    inp['cache_win_kv0'] = nrm((DEC_BATCH, l_win[0], 2, DIL_HEADS, HEAD_DIM))
    inp['cache_win_kv1'] = nrm((DEC_BATCH, l_win[1], 2, DIL_HEADS, HEAD_DIM))
    inp['cache_win_kv2'] = nrm((DEC_BATCH, l_win[2], 2, DIL_HEADS, HEAD_DIM))
    inp['mem_prompt'] = nrm((BATCH, N_MEM, D_MODEL))
    inp['g_mix'] = gain((DEPTH, D_MODEL))
    inp['w_in_a'] = nrm((n_a, D_MODEL, A_IN), D_MODEL ** -0.5)
    inp['w_conv_a'] = nrm((n_a, SSM_CONV, SSM_XBC), SSM_CONV ** -0.5)
    inp['b_conv_a'] = nrm((n_a, SSM_XBC), 0.02)
    dt0 = jnp.exp(jax.random.uniform(next(ks), (n_a, SSM_HEADS), f32, math.log(1e-3), math.log(1e-1)))
    inp['dt_bias_a'] = dt0 + jnp.log(-jnp.expm1(-dt0))
    inp['a_log_a'] = jnp.log(jax.random.uniform(next(ks), (n_a, SSM_HEADS), f32, 1.0, 16.0))
    inp['d_skip_a'] = 1.0 + nrm((n_a, SSM_HEADS), 0.1)
    inp['g_ssm_out_a'] = gain((n_a, SSM_INNER))
    inp['w_out_a'] = nrm((n_a, A_MIX, D_MODEL), A_MIX ** -0.5)
    inp['g_kv'] = gain((D_MODEL,))
    inp['w_kv'] = nrm((D_MODEL, KV_COLS), D_MODEL ** -0.5)
    inp['g_k_dil'] = gain((N_DIL, HEAD_DIM))
    inp['w_in_b'] = nrm((n_b, D_MODEL, B_IN), D_MODEL ** -0.5)
    inp['g_q_dil'] = gain((n_b, N_DIL, HEAD_DIM))
    inp['w_out_b'] = nrm((n_b, B_MIX, D_MODEL), B_MIX ** -0.5)
    inp['g_mem'] = gain((DEPTH, D_MODEL))
    inp['w_mem_kv'] = nrm((DEPTH, D_MODEL, 2 * MEM_W), D_MODEL ** -0.5)
    inp['g_mem_q'] = gain((DEPTH, MEM_HEAD_DIM))
    inp['g_mem_k'] = gain((DEPTH, MEM_HEAD_DIM))
    inp['g_ffn'] = gain((DEPTH, D_MODEL))
    inp['w_ffn_up'] = nrm((DEPTH, D_MODEL, 2 * D_FF), D_MODEL ** -0.5)
    inp['w_ffn_conv'] = nrm((DEPTH, FFN_CONV, D_FF), FFN_CONV ** -0.5)
    inp['b_ffn_conv'] = nrm((DEPTH, D_FF), 0.02)
    inp['w_ffn_down'] = nrm((DEPTH, D_FF, D_MODEL), D_FF ** -0.5)
    return inp


def reference(x_prompt, x_sample, state_ssm, state_ssm_conv, state_ffn_conv, cache_mem_kv, cache_win_kv0, cache_win_kv1, cache_win_kv2, mem_prompt, g_mix, w_in_a, w_conv_a, b_conv_a, dt_bias_a, a_log_a, d_skip_a, g_ssm_out_a, w_out_a, g_kv, w_kv, g_k_dil, w_in_b, g_q_dil, w_out_b, g_mem, w_mem_kv, g_mem_q, g_mem_k, g_ffn, w_ffn_up, w_ffn_conv, b_ffn_conv, w_ffn_down):
    p = dict(g_mix=g_mix, w_in_a=w_in_a, w_conv_a=w_conv_a, b_conv_a=b_conv_a, dt_bias_a=dt_bias_a, a_log_a=a_log_a, d_skip_a=d_skip_a, g_ssm_out_a=g_ssm_out_a, w_out_a=w_out_a, g_kv=g_kv, w_kv=w_kv, g_k_dil=g_k_dil, w_in_b=w_in_b, g_q_dil=g_q_dil, w_out_b=w_out_b, g_mem_q=g_mem_q, g_ffn=g_ffn, w_ffn_up=w_ffn_up, w_ffn_conv=w_ffn_conv, b_ffn_conv=b_ffn_conv, w_ffn_down=w_ffn_down)
    bp, sp = x_prompt.shape[0], x_prompt.shape[1]
    ds = x_sample.shape[1]
    dt_p = x_prompt.dtype
    mem_kv_p = jnp.stack([memory_kv(mem_prompt, g_mem[i], w_mem_kv[i], g_mem_k[i]) for i in range(DEPTH)], axis=0)
    ssm0 = jnp.zeros((N_A_LAYERS, bp, SSM_HEADS, SSM_HEAD_DIM, SSM_STATE), dt_p)
    conv0 = jnp.zeros((N_A_LAYERS, bp, SSM_CONV - 1, SSM_XBC), dt_p)
    ffn0 = jnp.zeros((DEPTH, bp, FFN_CONV - 1, D_FF), dt_p)
    win0 = [jnp.zeros((bp, 0, 2, DIL_HEADS, HEAD_DIM), dt_p) for _ in range(N_DIL)]
    y_p, ssm_p, conv_p, ffn_p, kv_p = trunk(x_prompt, jnp.arange(sp, dtype=jnp.int32), mem_kv_p, ssm0, conv0, ffn0, win0, p)
    pos_s = PAST_LEN + jnp.arange(ds, dtype=jnp.int32)
    y_s, ssm_s, conv_s, ffn_s, kv_s = trunk(x_sample, pos_s, cache_mem_kv, state_ssm, state_ssm_conv, state_ffn_conv, [cache_win_kv0, cache_win_kv1, cache_win_kv2], p)
    l0 = min(DIL_PATTERNS[0][0], sp)
    l1 = min(DIL_PATTERNS[1][0], sp)
    l2 = min(DIL_PATTERNS[2][0], sp)
    return (y_p, y_s, ssm_p, ssm_s, conv_p, conv_s, ffn_p, ffn_s, mem_kv_p, kv_p[:, sp - l0:, 0], kv_p[:, sp - l1:, 1], kv_p[:, sp - l2:, 2], kv_s[:, :, 0], kv_s[:, :, 1], kv_s[:, :, 2])
```

```python
import math
from contextlib import ExitStack
import numpy as np
import concourse.bass as bass
import concourse.mybir as mybir
from concourse.bass_utils import run_bass_kernel_spmd

F32 = mybir.dt.float32
BF16 = mybir.dt.bfloat16
AF = mybir.ActivationFunctionType
ALU = mybir.AluOpType
EPS = 1e-6
DIL = ((128, 1), (512, 4), (2048, 16))
ROPE_THETA = 500000.0


class Cfg:
    def __init__(s, D=4096, G=8, DFF=11008, SEQ=2048, NT=512, PAST=16384):
        s.D, s.G, s.DFF, s.SEQ, s.NT, s.PAST = D, G, DFF, SEQ, NT, PAST
        s.KT = D // 128
        s.INNER = G * 384
        s.HEADS = s.INNER // 64
        s.XBC = s.INNER + 2 * G * 128
        s.MHD = 256
        s.MEMW = 1024
        s.NZ = s.INNER // 128
        s.NXB = s.XBC // 128
        s.NQM = s.MEMW // 128
        s.NF = DFF // 128
        s.A_IN = s.INNER + s.XBC + s.HEADS + s.MEMW
        s.A_MIX = s.INNER + s.MEMW
        s.B_MIX = 1024 + s.MEMW
        s.WL = [min(w, SEQ) for w, _ in DIL]
        s.CL = [min(w, PAST) for w, _ in DIL]


class Buf:
    __slots__ = ("w", "r", "ps")

    def __init__(s, ps=False):
        s.w = None
        s.r = {}
        s.ps = ps


class Sync:
    ENG = ("pe", "act", "dve", "pool", "sp")

    def __init__(s, nc, es, ndma=24):
        s.nc = nc
        s.e = {"pe": nc.tensor, "act": nc.scalar, "dve": nc.vector, "pool": nc.gpsimd, "sp": nc.sync}
        s.sem = {k: es.enter_context(nc.semaphore("s_" + k)) for k in s.ENG}
        s.cnt = {k: 0 for k in s.ENG}
        s.pend = {k: False for k in s.ENG}
        s.nd = ndma
        for i in range(ndma):
            s.sem["d%d" % i] = es.enter_context(nc.semaphore("s_d%d" % i))
            s.cnt["d%d" % i] = 0
        s.seen = {k: {} for k in s.ENG}
        s.rr = {"sp": 0, "pool": 0, "act": 0}

    def _wait(s, E, deps):
        need = {}
        for k, v in deps:
            if v > need.get(k, 0):
                need[k] = v
        for k, v in need.items():
            if k == E and E in ("pe", "sp"):
                continue
            if s.seen[E].get(k, 0) >= v:
                continue
            assert v <= s.cnt[k] + (1 if s.pend.get(k, False) else 0), (E, k, v, s.cnt[k])
            if s.pend.get(k, False) and v > s.cnt[k]:
                raise RuntimeError("wait on pending (non-incremented) op %s->%s" % (k, E))
            s.e[E].wait_ge(s.sem[k], v)
            s.seen[E][k] = v

    def _deps(s, r, w):
        deps = []
        for b in r:
            if b.w is not None:
                deps.append(b.w)
        for b in w:
            if b.w is not None:
                deps.append(b.w)
            deps.extend(b.r.items())
        return deps

    def _mark(s, ev, r, w):
        k, v = ev
        for b in r:
            if b.r.get(k, 0) < v:
                b.r[k] = v
        for b in w:
            b.w = ev
            b.r = {}

    def op(s, E, emit, r=(), w=(), inc=True):
        if any(b.ps for b in r):
            w = list(w) + [b for b in r if b.ps]
            r = [b for b in r if not b.ps]
        s._wait(E, s._deps(r, w))
        ins = emit()
        if inc:
            s.cnt[E] += 1
            ins.then_inc(s.sem[E], 1)
            s.pend[E] = False
            ev = (E, s.cnt[E])
        else:
            s.pend[E] = True
            ev = (E, s.cnt[E] + 1)
        s._mark(ev, r, w)
        return ins

    def dma(s, Q, out, in_, r=(), w=(), **kw):
        deps = s._deps(r, w)
        half = s.nd // 2
        base = half if Q == "pool" else 0
        d = "d%d" % (base + s.rr[Q] % half)
        s.rr[Q] += 1
        deps.append((d, s.cnt[d]))
        s._wait(Q, deps)
        s.cnt[d] += 16
        s.e[Q].dma_start(out=out, in_=in_, **kw).then_inc(s.sem[d], 16)
        s._mark((d, s.cnt[d]), r, w)

    def flush(s, E):
        if s.pend[E]:
            s.op(E, lambda: s.e[E].nop(), inc=True)

    def barrier(s):
        for E in s.ENG:
            assert not s.pend[E], E
        allev = [(k, v) for k, v in s.cnt.items() if v > 0]
        for E in s.ENG:
            s._wait(E, allev)


class T:
    def __init__(s, t, psum=False):
        s.t = t
        s.bufs = {}
        s.psum = psum

    def b(s, *key):
        if key not in s.bufs:
            s.bufs[key] = Buf(s.psum)
        return s.bufs[key]

    def bs(s, keys):
        return [s.b(k) for k in keys]

    def __getitem__(s, k):
        return s.t[k]


def blockify(W, blocks):
    K = W.shape[0]
    KT = K // 128
    out = np.zeros((len(blocks), 128, KT, 128), np.float32)
    for i, (c0, M) in enumerate(blocks):
        out[i, :, :, :M] = W[:, c0:c0 + M].reshape(KT, 128, M).transpose(1, 0, 2)
    return out


def pp(v):
    v = np.asarray(v, np.float32)
    return np.ascontiguousarray(v.reshape(-1, 128).T)


def rope_tables(pos):
    half = 16
    inv = np.exp(-(2.0 * np.arange(half, dtype=np.float32) / 32.0) * np.float32(math.log(ROPE_THETA))).astype(np.float32)
    ang = pos.astype(np.float32)[:, None] * inv[None, :]
    cos = np.cos(ang).astype(np.float32).T
    sin = np.sin(ang).astype(np.float32).T
    n = pos.shape[0]
    cf = np.ones((128, n), np.float32)
    sf = np.zeros((128, n), np.float32)
    cf[0:16] = cos
    cf[16:32] = cos
    sf[0:16] = sin
    sf[16:32] = sin
    return cf, sf


def build(cfg, do_prompt=True, do_sample=True, dbg_stop=99):
    c_ = cfg
    nc = bass.Bass("TRN2", target_bir_lowering=False)
    D, KT, NT, SEQ = c_.D, c_.KT, c_.NT, c_.SEQ
    NTILES = SEQ // NT
    NPASS = NTILES + 1

    def din(name, shape):
        return T(nc.dram_tensor(name, list(shape), F32, kind="ExternalInput").ap())

    def dout(name, shape):
        return T(nc.dram_tensor(name, list(shape), F32, kind="ExternalOutput").ap())

    def dscr(name, shape):
        return T(nc.dram_tensor(name, list(shape), F32, kind="Internal").ap())

    I = {}
    I["xp"] = din("xp", [SEQ, D])
    I["xs"] = din("xs", [1, D])
    I["st_ssm"] = din("st_ssm", [c_.INNER, 128])
    I["st_conv"] = din("st_conv", [3, c_.XBC])
    I["st_ffn"] = din("st_ffn", [2, 2, c_.DFF])
    I["c_mem"] = din("c_mem", [2, 256, 2 * c_.MEMW])
    for g in range(3):
        I["c_win%d" % g] = din("c_win%d" % g, [c_.CL[g], 2048])
    I["memp"] = din("memp", [256, D])
    colsA = [(j * 128, 128) for j in range(c_.NZ + c_.NXB)] + [(c_.INNER + c_.XBC, c_.HEADS)] + \
            [(c_.INNER + c_.XBC + c_.HEADS + j * 128, 128) for j in range(c_.NQM)]
    NBA = len(colsA)
    I["w_in_a"] = din("w_in_a", [NBA, 128, KT, 128])
    I["w_out_a"] = din("w_out_a", [KT, 128, c_.A_MIX // 128, 128])
    I["w_kv"] = din("w_kv", [48, 128, KT, 128])
    I["w_in_b"] = din("w_in_b", [24 + c_.NQM, 128, KT, 128])
    I["w_out_b"] = din("w_out_b", [KT, 128, c_.B_MIX // 128, 128])
    I["w_mem_kv"] = din("w_mem_kv", [2, 2 * c_.NQM, 128, KT, 128])
    I["w_up"] = din("w_up", [2, 2 * c_.NF, 128, KT, 128])
    I["w_down"] = din("w_down", [2, KT, 128, c_.NF, 128])
    I["g_mix"] = din("g_mix", [2, 128, KT])
    I["g_ffn"] = din("g_ffn", [2, 128, KT])
    I["g_kv"] = din("g_kv", [128, KT])
    I["g_mem"] = din("g_mem", [2, 128, D])
    I["g_mem_k"] = din("g_mem_k", [2, 128, c_.MHD])
    I["g_mem_q"] = din("g_mem_q", [2, 128, c_.MHD // 128])
    I["g_mem_kp"] = din("g_mem_kp", [2, 128, c_.MHD // 128])
    I["wconv"] = din("wconv", [128, c_.NXB, 4])
    I["bconv"] = din("bconv", [128, c_.NXB])
    I["dtb"] = din("dtb", [c_.HEADS, 1])
    I["alog"] = din("alog", [c_.HEADS, 1])
    I["dsk"] = din("dsk", [128, c_.NZ])
    I["gout"] = din("gout", [128, c_.NZ])
    I["gk"] = din("gk", [128, 3])
    I["gq"] = din("gq", [128, 3])
    I["wfc"] = din("wfc", [2, 128, c_.NF, 3])
    I["bfc"] = din("bfc", [2, 128, c_.NF])
    I["cosf"] = din("cosf", [NPASS, 128, NT])
    I["sinf"] = din("sinf", [NPASS, 128, NT])
    I["consts"] = din("consts", [5, 128, 128])
    O = {}
    O["y_p"] = dout("y_p", [SEQ, D])
    O["y_s"] = dout("y_s", [1, D])
    O["ssm_p"] = dout("ssm_p", [c_.INNER, 128])
    O["ssm_s"] = dout("ssm_s", [c_.INNER, 128])
    O["conv_p"] = dout("conv_p", [3, c_.XBC])
    O["conv_s"] = dout("conv_s", [3, c_.XBC])
    O["ffn_p"] = dout("ffn_p", [2, 2, c_.DFF])
    O["ffn_s"] = dout("ffn_s", [2, 2, c_.DFF])
    O["memkv_p"] = dout("memkv_p", [2, 256, 2 * c_.MEMW])
    for g in range(3):
        O["win%d_p" % g] = dout("win%d_p" % g, [c_.WL[g], 2048])
        O["win%d_s" % g] = dout("win%d_s" % g, [1, 2048])
    X = {}
    X["xres"] = dscr("xres", [KT, 128, NT])
    X["kvs"] = dscr("kvs", [SEQ, 6144])
    X["kvs_s"] = dscr("kvs_s", [1, 6144])

    es = ExitStack()
    with es:
        es.enter_context(nc.allow_non_contiguous_dma(reason="single-token (N=1) sample pass moves 4-byte columns"))
        S = Sync(nc, es)

        uid = [0]

        def sb(name, shape, dt=F32, st=None):
            uid[0] += 1
            return T((st or es).enter_context(nc.sbuf_tensor("sb%d_%s" % (uid[0], name), list(shape), dt)))

        def ps(name, shape, dt=F32, st=None):
            uid[0] += 1
            return T((st or es).enter_context(nc.psum_tensor("ps%d_%s" % (uid[0], name), list(shape), dt)), psum=True)

        V = lambda f, r=(), w=(): S.op("dve", f, r, w)
        A = lambda f, r=(), w=(): S.op("act", f, r, w)
        PL = lambda f, r=(), w=(): S.op("pool", f, r, w)
        PE = lambda f, r=(), w=(), inc=True: S.op("pe", f, r, w, inc)

        cst = sb("cst", [128, 5, 128])
        cstb = sb("cstb", [128, 5, 128], BF16)
        S.dma("sp", cst[:], I["consts"].t.rearrange("k p m -> p k m"), w=[cst.b()])
        V(lambda: nc.vector.tensor_copy(out=cstb[:], in_=cst[:]), [cst.b()], [cstb.b()])
        ident, ones, Um, Lm, RTm = (cst[:, i, :] for i in range(5))
        identb, onesb = cstb[:, 0, :], cstb[:, 1, :]
        CB = [cst.b(), cstb.b()]

        def ld(name, src, shape):
            t = sb(name, shape)
            S.dma("sp", t[:], src, w=[t.b()])
            return t

        gmix = [ld("gmix%d" % i, I["g_mix"].t[i], [128, KT]) for i in range(2)]
        gffn = [ld("gffn%d" % i, I["g_ffn"].t[i], [128, KT]) for i in range(2)]
        gkv = ld("gkv", I["g_kv"].t, [128, KT])
        gmq = [ld("gmq%d" % i, I["g_mem_q"].t[i], [128, c_.MHD // 128]) for i in range(2)]
        gmkp = [ld("gmkp%d" % i, I["g_mem_kp"].t[i], [128, c_.MHD // 128]) for i in range(2)]
        wconv = ld("wconv", I["wconv"].t, [128, c_.NXB, 4])
        bconv = ld("bconv", I["bconv"].t, [128, c_.NXB])
        dtb = ld("dtb", I["dtb"].t, [c_.HEADS, 1])
        alog = ld("alog", I["alog"].t, [c_.HEADS, 1])
        dsk = ld("dsk", I["dsk"].t, [128, c_.NZ])
        gout = ld("gout", I["gout"].t, [128, c_.NZ])
        gk = ld("gk", I["gk"].t, [128, 3])
        gq = ld("gq", I["gq"].t, [128, 3])
        wfc = [ld("wfc%d" % i, I["wfc"].t[i], [128, c_.NF, 3]) for i in range(2)]
        bfc = [ld("bfc%d" % i, I["bfc"].t[i], [128, c_.NF]) for i in range(2)]
        aneg = sb("aneg", [c_.HEADS, 1])
        A(lambda: nc.scalar.activation(out=aneg[:], in_=alog[:], func=AF.Exp), [alog.b()], [aneg.b()])
        V(lambda: nc.vector.tensor_scalar(out=aneg[:], in0=aneg[:], scalar1=-1.0, scalar2=None, op0=ALU.mult),
          [aneg.b()], [aneg.b()])
        Sst = sb("Sst", [128, c_.INNER])
        Sbf = sb("Sbf", [128, c_.INNER], BF16)
        ctail = sb("ctail", [128, c_.NXB, 3])
        ftail = [sb("ftail%d" % i, [128, c_.NF, 2]) for i in range(2)]
        cosf = sb("cosf", [128, NT])
        sinf = sb("sinf", [128, NT])
        KC = 16
        NS, NB = 3, 2
        wst = [sb("wst%d" % i, [128, KC, 128]) for i in range(NS)]
        wbf = [sb("wbf%d" % i, [128, KC, 128], BF16) for i in range(NB)]
        wctr = [0]

        xres = X["xres"]

        def gemm(wd, wdb, nblk, nkt, mov, movb, N, psums, evac, Ms=None):
            chunks = []
            for j in range(nblk):
                for k0 in range(0, nkt, KC):
                    chunks.append((j, k0, min(KC, nkt - k0)))
            ids = []
            for (j, k0, kc) in chunks:
                ids.append(wctr[0])
                wctr[0] += 1

            def emit_dma(i):
                j, k0, kc = chunks[i]
                t = wst[ids[i] % NS]
                S.dma("sp", t[:, 0:kc, :], wd[j, :, k0:k0 + kc, :], r=[wdb], w=[t.b()])

            def emit_cast(i):
                j, k0, kc = chunks[i]
                t = wst[ids[i] % NS]
                u = wbf[ids[i] % NB]
                A(lambda: nc.scalar.activation(out=u[:, 0:kc, :], in_=t[:, 0:kc, :], func=AF.Copy), [t.b()], [u.b()])

            LD, LC = 2, 1
            for i in range(min(LD, len(chunks))):
                emit_dma(i)
            for i in range(min(LC, len(chunks))):
                emit_cast(i)
            for i, (j, k0, kc) in enumerate(chunks):
                if i + LD < len(chunks):
                    emit_dma(i + LD)
                if i + LC < len(chunks):
                    emit_cast(i + LC)
                u = wbf[ids[i] % NB]
                pst = psums[j % len(psums)]
                M = 128 if Ms is None else Ms[j]
                for k in range(kc):
                    kt = k0 + k
                    last = (kt == nkt - 1)
                    PE(lambda: nc.tensor.matmul(pst[0:M, 0:N], lhsT=u[:, k, 0:M], rhs=mov(kt),
                                                start=(kt == 0), stop=last),
                       [u.b(), movb(kt)], [pst.b()], inc=(last or k == kc - 1))
                if k0 + kc == nkt:
                    evac(j, pst, M)

        def norm_stream(gp, hT, N, pstat, xb, tmp, rstd):
            for kt in range(KT):
                x_ = xb[kt % 4]
                S.dma("pool", x_[:, 0:N], xres.t[kt, :, 0:N], r=[xres.b(kt)], w=[x_.b()])
                sq = tmp[kt % 2]
                A(lambda: nc.scalar.activation(out=sq[:, 0:N], in_=x_[:, 0:N], func=AF.Square), [x_.b()], [sq.b()])
                PE(lambda: nc.tensor.matmul(pstat[:, 0:N], lhsT=onesb, rhs=sq[:, 0:N], start=(kt == 0),
                                            stop=(kt == KT - 1)), [sq.b()] + CB, [pstat.b()], inc=True)
            A(lambda: nc.scalar.activation(out=rstd[:, 0:N], in_=pstat[:, 0:N], func=AF.Sqrt, scale=1.0 / D, bias=EPS),
              [pstat.b()], [rstd.b()])
            V(lambda: nc.vector.reciprocal(out=rstd[:, 0:N], in_=rstd[:, 0:N]), [rstd.b()], [rstd.b()])
            for kt in range(KT):
                x_ = xb[kt % 4]
                S.dma("pool", x_[:, 0:N], xres.t[kt, :, 0:N], r=[xres.b(kt)], w=[x_.b()])
                V(lambda: nc.vector.scalar_tensor_tensor(out=hT[:, kt, 0:N], in0=x_[:, 0:N],
                                                         scalar=gp[:, kt:kt + 1], in1=rstd[:, 0:N],
                                                         op0=ALU.mult, op1=ALU.mult),
                  [x_.b(), gp.b(), rstd.b()], [hT.b(kt)])

        def rstd_from(pst, out, N, n, parts=128):
            A(lambda: nc.scalar.activation(out=out[0:parts, 0:N], in_=pst[0:parts, 0:N], func=AF.Sqrt,
                                           scale=1.0 / n, bias=EPS), [pst.b()], [out.b()])
            V(lambda: nc.vector.reciprocal(out=out[0:parts, 0:N], in_=out[0:parts, 0:N]), [out.b()], [out.b()])

        def load_xres(XB, N):
            S.dma("pool", XB[:, :, 0:N], xres.t.rearrange("k p n -> p k n")[:, :, 0:N],
                  r=xres.bs(range(KT)), w=XB.bs(range(KT)))

        def make_resid_evac(N, xin, xo):
            ctr = [0]

            def evac(j, pst, M):
                i = ctr[0] % 2
                ctr[0] += 1
                S.dma("pool", xin[i][:, 0:N], xres.t[j, :, 0:N], r=[xres.b(j)], w=[xin[i].b()])
                V(lambda: nc.vector.tensor_tensor(out=xo[i][:, 0:N], in0=pst[:, 0:N], in1=xin[i][:, 0:N], op=ALU.add),
                  [pst.b(), xin[i].b()], [xo[i].b()])
                S.dma("pool", xres.t[j, :, 0:N], xo[i][:, 0:N], r=[xo[i].b()], w=[xres.b(j)])
            return evac

        def mem_kv_prompt():
            with ExitStack() as st:
                mtok = sb("mtok", [128, D], st=st)
                gm = sb("gm", [128, D], st=st)
                junk = sb("mjunk", [128, D], st=st)
                ss = sb("mss", [128, 1], st=st)
                memT = sb("memT", [128, KT, 256], BF16, st=st)
                kvf = sb("mkvf", [128, 2, 256], st=st)
                tmp = [sb("mtmp%d" % i, [128, 256], BF16, st=st) for i in range(2)]
                rs = sb("mrs", [128, 256], st=st)
                stg = sb("mstg", [128, 2, 128], st=st)
                gmk = sb("gmk", [128, c_.MHD], st=st)
                ptr = ps("mptr", [128, 512], st=st)
                pg = [ps("mpg%d" % i, [128, 512], st=st) for i in range(2)]
                pn = ps("mpn", [128, 512], st=st)
                pt2 = ps("mpt2", [128, 512], st=st)
                nb = c_.MHD // 128
                for li in range(2):
                    S.dma("sp", gm[:], I["g_mem"].t[li], w=[gm.b()])
                    S.dma("sp", gmk[:], I["g_mem_k"].t[li], w=[gmk.b()])
                    for mt in range(2):
                        S.dma("sp", mtok[:], I["memp"].t[mt * 128:(mt + 1) * 128, :], w=[mtok.b()])
                        import os as _os
                        lvl = int(_os.environ.get("KLVL", "9"))
                        if lvl < 1:
                            continue
                        A(lambda: nc.scalar.activation(out=junk[:], in_=mtok[:], func=AF.Square), [mtok.b()], [junk.b()])
                        if lvl < 2:
                            continue
                        V(lambda: nc.vector.tensor_reduce(out=ss[:], in_=junk[:], axis=mybir.AxisListType.X, op=ALU.add),
                          [junk.b()], [ss.b()])
                        if lvl < 3:
                            continue
                        A(lambda: nc.scalar.activation(out=ss[:], in_=ss[:], func=AF.Sqrt, scale=1.0 / D, bias=EPS),
                          [ss.b()], [ss.b()])
                        V(lambda: nc.vector.reciprocal(out=ss[:], in_=ss[:]), [ss.b()], [ss.b()])
                        if lvl < 4:
                            continue
                        V(lambda: nc.vector.scalar_tensor_tensor(out=mtok[:], in0=mtok[:], scalar=ss[:, 0:1], in1=gm[:],
                                                                 op0=ALU.mult, op1=ALU.mult),
                          [mtok.b(), ss.b(), gm.b()], [mtok.b()])
                        if lvl < 5:
                            continue
                        for k4 in range(0, KT, 4):
                            for q in range(4):
                                kt = k4 + q
                                PE(lambda: nc.tensor.transpose(ptr[:, q * 128:(q + 1) * 128], mtok[:, kt * 128:(kt + 1) * 128], ident),
                                   [mtok.b()] + CB, [ptr.b()], inc=(q == 3))
                            V(lambda: nc.vector.tensor_copy(out=memT[:, k4:k4 + 4, mt * 128:(mt + 1) * 128],
                                                            in_=ptr[:].rearrange("p (q m) -> p q m", q=4)),
                              [ptr.b()], memT.bs(range(k4, k4 + 4)))
                    nblk = 2 * c_.NQM
                    outd = O["memkv_p"]

                    def put_tok(src, col0):
                        for mt in range(2):
                            PE(lambda: nc.tensor.transpose(pt2[:, mt * 128:(mt + 1) * 128], src[:, mt * 128:(mt + 1) * 128], ident),
                               [kvf.b()] + CB, [pt2.b()], inc=(mt == 1))
                        V(lambda: nc.vector.tensor_copy(out=stg[:], in_=pt2[:, 0:256].rearrange("p (a m) -> p a m", a=2)),
                          [pt2.b()], [stg.b()])
                        S.dma("pool", outd.t[li, :, col0:col0 + 128].rearrange("(a p) m -> p a m", a=2), stg[:],
                              r=[stg.b()], w=[outd.b(li)])

                    def evac(j, pst, M):
                        if j < c_.NQM:
                            bi = j % nb
                            V(lambda: nc.vector.tensor_copy(out=kvf[:, bi, :], in_=pst[:, 0:256]), [pst.b()], [kvf.b()])
                            sq = tmp[bi % 2]
                            A(lambda: nc.scalar.activation(out=sq[:], in_=pst[:, 0:256], func=AF.Square), [pst.b()], [sq.b()])
                            if bi == nb - 1:
                                for b2 in range(nb):
                                    PE(lambda: nc.tensor.matmul(pn[:, 0:256], lhsT=onesb, rhs=tmp[b2 % 2][:], start=(b2 == 0), stop=(b2 == nb - 1)),
                                       [tmp[b2 % 2].b()] + CB, [pn.b()], inc=(b2 == nb - 1))
                                rstd_from(pn, rs, 256, c_.MHD)
                                for b2 in range(nb):
                                    V(lambda: nc.vector.scalar_tensor_tensor(out=kvf[:, b2, :], in0=kvf[:, b2, :], scalar=gmkp[li][:, b2:b2 + 1],
                                                                             in1=rs[:, 0:256], op0=ALU.mult, op1=ALU.mult),
                                      [kvf.b(), rs.b(), gmkp[li].b()], [kvf.b()])
                                    jj = j - (nb - 1) + b2
                                    put_tok(kvf[:, b2, :], jj * 128)
                        else:
                            V(lambda: nc.vector.tensor_copy(out=kvf[:, 0, :], in_=pst[:, 0:256]), [pst.b()], [kvf.b()])
                            put_tok(kvf[:, 0, :], j * 128)

                    import os as _os
                    ksub = int(_os.environ.get("KSUB", "9"))

                    def evac_dbg(j, pst, M):
                        V(lambda: nc.vector.tensor_copy(out=kvf[:, 0, :], in_=pst[:, 0:256]), [pst.b()], [kvf.b()])
                    if ksub >= 2:
                        gemm(I["w_mem_kv"].t[li], I["w_mem_kv"].b(), nblk, KT, lambda kt: memT[:, kt, :], lambda kt: memT.b(kt),
                             256, pg, evac if ksub >= 3 else evac_dbg)
                S.barrier()

        def trunk(pi, N, c, x_src, y_dst, kv_dst, kv_base, memkv_src, final, sample):
            nch = N // c
            t0 = kv_base
            with ExitStack() as st:
                XB = sb("XB0", [128, KT, NT], st=st)
                xtok = sb("xtok", [128, D], st=st)
                ptr = [ps("p0tr%d" % i, [128, 512], st=st) for i in range(2)]
                for ci in range(nch):
                    S.dma("sp", xtok[0:c, :], x_src.t[ci * c:(ci + 1) * c, :], w=[xtok.b()])
                    for k4 in range(0, KT, 4):
                        p = ptr[(k4 // 4) % 2]
                        for q in range(4):
                            kt = k4 + q
                            PE(lambda: nc.tensor.transpose(p[:, q * 128:q * 128 + c], xtok[0:c, kt * 128:(kt + 1) * 128], ident[0:c, 0:c]),
                               [xtok.b()] + CB, [p.b()], inc=(q == 3))
                        V(lambda: nc.vector.tensor_copy(out=XB[:, k4:k4 + 4, ci * c:(ci + 1) * c],
                                                        in_=p[:].rearrange("p (q m) -> p q m", q=4)[:, :, 0:c]),
                          [p.b()], XB.bs(range(k4, k4 + 4)))
                S.dma("pool", xres.t.rearrange("k p n -> p k n")[:, :, 0:N], XB[:, :, 0:N],
                      r=XB.bs(range(KT)), w=xres.bs(range(KT)))
                S.barrier()

            S.dma("sp", cosf[:], I["cosf"].t[pi], w=[cosf.b()])
            S.dma("sp", sinf[:], I["sinf"].t[pi], w=[sinf.b()])

            def norm_phase(gp, hT, st0):
                with ExitStack() as st:
                    xb = [sb("nxb%d" % i, [128, NT], st=st) for i in range(4)]
                    tmp = [sb("ntmp%d" % i, [128, NT], BF16, st=st) for i in range(2)]
                    rstd = sb("nrstd", [128, NT], st=st)
                    pstat = ps("npstat", [128, 512], st=st)
                    norm_stream(gp, hT, N, pstat, xb, tmp, rstd)
                    S.barrier()

            def normrope(pst, gcol, outap, outb, wk, pn, pr):
                raw, sq, rs, t1 = wk
                V(lambda: nc.vector.tensor_copy(out=raw[:, 0:N], in_=pst[:, 0:N]), [pst.b()], [raw.b()])
                A(lambda: nc.scalar.activation(out=sq[:, 0:N], in_=pst[:, 0:N], func=AF.Square), [pst.b()], [sq.b()])
                PE(lambda: nc.tensor.matmul(pn[:, 0:N], lhsT=ones, rhs=sq[:, 0:N], start=True, stop=True),
                   [sq.b()] + CB, [pn.b()])
                rstd_from(pn, rs, N, 128)
                V(lambda: nc.vector.scalar_tensor_tensor(out=raw[:, 0:N], in0=raw[:, 0:N], scalar=gcol, in1=rs[:, 0:N],
                                                         op0=ALU.mult, op1=ALU.mult),
                  [raw.b(), rs.b(), gk.b(), gq.b()], [raw.b()])
                PE(lambda: nc.tensor.matmul(pr[:, 0:N], lhsT=RTm, rhs=raw[:, 0:N], start=True, stop=True),
                   [raw.b()] + CB, [pr.b()])
                V(lambda: nc.vector.tensor_tensor(out=t1[:, 0:N], in0=pr[:, 0:N], in1=sinf[:, 0:N], op=ALU.mult),
                  [pr.b(), sinf.b()], [t1.b()])
                V(lambda: nc.vector.tensor_tensor(out=raw[:, 0:N], in0=raw[:, 0:N], in1=cosf[:, 0:N], op=ALU.mult),
                  [raw.b(), cosf.b()], [raw.b()])
                V(lambda: nc.vector.tensor_tensor(out=outap, in0=raw[:, 0:N], in1=t1[:, 0:N], op=ALU.add),
                  [raw.b(), t1.b()], [outb])

            def mem_attn(li, qm, ymem, st):
                nb = c_.MHD // 128
                kvm = sb("kvm", [128, 2, c_.MEMW], st=st)
                KTb = sb("KTb", [128, nb, 4, 256], BF16, st=st)
                Vb = sb("Vb", [128, 2, c_.MEMW], BF16, st=st)
                qn = sb("qn", [128, c_.NQM, NT], BF16, st=st)
                sq = [sb("masq%d" % i, [128, NT], BF16, st=st) for i in range(2)]
                rs = sb("mars", [128, NT], st=st)
                PT = sb("maPT", [128, 2, NT], BF16, st=st)
                rden = sb("marden", [128, NT], st=st)
                ptk = ps("maptk", [128, 512], st=st)
                pn = ps("mapn", [128, 512], st=st)
                pS = [ps("mapS%d" % i, [128, 512], st=st) for i in range(2)]
                pd = ps("mapd", [128, 512], st=st)
                pO = [ps("mapO%d" % i, [128, 512], st=st) for i in range(2)]
                S.dma("sp", kvm[:], memkv_src.t[li].rearrange("(a p) m -> p a m", a=2)[:, :, 0:c_.MEMW], r=[memkv_src.b(li)], w=[kvm.b()])
                for hd in range(4):
                    for bl in range(nb):
                        for mt in range(2):
                            col = hd * c_.MHD + bl * 128
                            PE(lambda: nc.tensor.transpose(ptk[:, mt * 128:(mt + 1) * 128], kvm[:, mt, col:col + 128], ident),
                               [kvm.b()] + CB, [ptk.b()], inc=(mt == 1))
                        V(lambda: nc.vector.tensor_copy(out=KTb[:, bl, hd, :], in_=ptk[:, 0:256]), [ptk.b()], [KTb.b()])
                S.dma("sp", kvm[:], memkv_src.t[li].rearrange("(a p) m -> p a m", a=2)[:, :, c_.MEMW:2 * c_.MEMW], r=[memkv_src.b(li)], w=[kvm.b()])
                A(lambda: nc.scalar.activation(out=Vb[:], in_=kvm[:], func=AF.Copy), [kvm.b()], [Vb.b()])
                for hd in range(4):
                    for bl in range(nb):
                        j = hd * nb + bl
                        s_ = sq[bl % 2]
                        A(lambda: nc.scalar.activation(out=s_[:, 0:N], in_=qm[:, j, 0:N], func=AF.Square), [qm.b(j)], [s_.b()])
                        PE(lambda: nc.tensor.matmul(pn[:, 0:N], lhsT=onesb, rhs=s_[:, 0:N], start=(bl == 0), stop=(bl == nb - 1)),
                           [s_.b()] + CB, [pn.b()])
                    rstd_from(pn, rs, N, c_.MHD)
                    for bl in range(nb):
                        j = hd * nb + bl
                        V(lambda: nc.vector.scalar_tensor_tensor(out=qn[:, j, 0:N], in0=qm[:, j, 0:N], scalar=gmq[li][:, bl:bl + 1],
                                                                 in1=rs[:, 0:N], op0=ALU.mult, op1=ALU.mult),
                          [qm.b(j), rs.b(), gmq[li].b()], [qn.b(j)])
                scale = c_.MHD ** -0.5
                for hd in range(4):
                    for mt in range(2):
                        p = pS[mt]
                        for bl in range(nb):
                            PE(lambda: nc.tensor.matmul(p[:, 0:N], lhsT=KTb[:, bl, hd, mt * 128:(mt + 1) * 128], rhs=qn[:, hd * nb + bl, 0:N],
                                                        start=(bl == 0), stop=(bl == nb - 1)),
                               [KTb.b(), qn.b(hd * nb + bl)], [p.b()], inc=(bl == nb - 1))
                        A(lambda: nc.scalar.activation(out=PT[:, mt, 0:N], in_=p[:, 0:N], func=AF.Exp, scale=scale), [p.b()], [PT.b(mt)])
                    for mt in range(2):
                        PE(lambda: nc.tensor.matmul(pd[:, 0:N], lhsT=onesb, rhs=PT[:, mt, 0:N], start=(mt == 0), stop=(mt == 1)),
                           [PT.b(mt)] + CB, [pd.b()], inc=(mt == 1))
                    V(lambda: nc.vector.reciprocal(out=rden[:, 0:N], in_=pd[:, 0:N]), [pd.b()], [rden.b()])
                    for bl in range(nb):
                        p = pO[bl % 2]
                        for mt in range(2):
                            col = hd * c_.MHD + bl * 128
                            PE(lambda: nc.tensor.matmul(p[:, 0:N], lhsT=Vb[:, mt, col:col + 128], rhs=PT[:, mt, 0:N],
                                                        start=(mt == 0), stop=(mt == 1)),
                               [Vb.b(), PT.b(mt)], [p.b()], inc=(mt == 1))
                        j = hd * nb + bl
                        V(lambda: nc.vector.tensor_tensor(out=ymem[:, j, 0:N], in0=p[:, 0:N], in1=rden[:, 0:N], op=ALU.mult),
                          [p.b(), rden.b()], [ymem.b(j)])

            def ffn(li):
                with ExitStack() as st:
                    hT = sb("fhT", [128, KT, NT], BF16, st=st)
                    norm_phase(gffn[li], hT, st)
                    act = sb("fact", [128, c_.NF, NT], BF16, st=st)
                    pad = [sb("fpad%d" % i, [128, NT + 2], st=st) for i in range(2)]
                    acc = [sb("facc%d" % i, [128, NT], st=st) for i in range(2)]
                    gs = [sb("fgs%d" % i, [128, NT], st=st) for i in range(2)]
                    pg = [ps("fpg%d" % i, [128, 512], st=st) for i in range(4)]
                    ft = ftail[li]

                    def evac_up(j, pst, M):
                        jf = j // 2
                        i = jf % 2
                        if j % 2 == 0:
                            pd_, ac = pad[i], acc[i]
                            V(lambda: nc.vector.tensor_copy(out=pd_[:, 0:2], in_=ft[:, jf, :]), [ft.b(jf)], [pd_.b()])
                            V(lambda: nc.vector.tensor_copy(out=pd_[:, 2:2 + N], in_=pst[:, 0:N]), [pst.b()], [pd_.b()])
                            V(lambda: nc.vector.tensor_copy(out=ft[:, jf, :], in_=pd_[:, N:N + 2]), [pd_.b()], [ft.b(jf)])
                            V(lambda: nc.vector.tensor_scalar(out=ac[:, 0:N], in0=pd_[:, 0:N], scalar1=wfc[li][:, jf, 0:1],
                                                              scalar2=bfc[li][:, jf:jf + 1], op0=ALU.mult, op1=ALU.add),
                              [pd_.b(), wfc[li].b(), bfc[li].b()], [ac.b()])
                            for k in (1, 2):
                                V(lambda: nc.vector.scalar_tensor_tensor(out=ac[:, 0:N], in0=pd_[:, k:k + N], scalar=wfc[li][:, jf, k:k + 1],
                                                                         in1=ac[:, 0:N], op0=ALU.mult, op1=ALU.add),
                                  [pd_.b(), ac.b()], [ac.b()])
                            A(lambda: nc.scalar.activation(out=gs[i][:, 0:N], in_=ac[:, 0:N], func=AF.Silu), [ac.b()], [gs[i].b()])
                        else:
                            V(lambda: nc.vector.tensor_tensor(out=act[:, jf, 0:N], in0=pst[:, 0:N], in1=gs[i][:, 0:N], op=ALU.mult),
                              [pst.b(), gs[i].b()], [act.b(jf)])

                    gemm(I["w_up"].t[li], I["w_up"].b(), 2 * c_.NF, KT, lambda kt: hT[:, kt, 0:N], lambda kt: hT.b(kt), N, pg, evac_up)
                    xin = [sb("fxin%d" % i, [128, NT], st=st) for i in range(2)]
                    xo = [sb("fxo%d" % i, [128, NT], st=st) for i in range(2)]
                    gemm(I["w_down"].t[li], I["w_down"].b(), KT, c_.NF, lambda kt: act[:, kt, 0:N], lambda kt: act.b(kt), N, pg[0:2],
                         make_resid_evac(N, xin, xo))
                    S.barrier()

            with ExitStack() as st:
                sz = sb("asz", [128, c_.NZ, NT], BF16, st=st)
                xbc = sb("axbc", [128, c_.NXB, NT], BF16, st=st)
                qm = sb("aqm", [128, c_.NQM, NT], st=st)
                ymem = sb("aymem", [128, c_.NQM, NT], BF16, st=st)
                dtT = sb("adtT", [c_.HEADS, NT], st=st)
                laT = sb("alaT", [c_.HEADS, NT], st=st)
                with ExitStack() as st2:
                    hT = sb("ahT", [128, KT, NT], BF16, st=st2)
                    norm_phase(gmix[0], hT, st2)
                    pad = [sb("apad%d" % i, [128, NT + 3], st=st2) for i in range(2)]
                    acc = [sb("aacc%d" % i, [128, NT], st=st2) for i in range(2)]
                    pg = [ps("apg%d" % i, [128, 512], st=st2) for i in range(3)]
                    cc = [0]

                    def evacA(j, pst, M):
                        if j < c_.NZ:
                            A(lambda: nc.scalar.activation(out=sz[:, j, 0:N], in_=pst[:, 0:N], func=AF.Silu), [pst.b()], [sz.b(j)])
                        elif j < c_.NZ + c_.NXB:
                            jb = j - c_.NZ
                            i = cc[0] % 2
                            cc[0] += 1
                            pd_, ac = pad[i], acc[i]
                            V(lambda: nc.vector.tensor_copy(out=pd_[:, 0:3], in_=ctail[:, jb, :]), [ctail.b(jb)], [pd_.b()])
                            V(lambda: nc.vector.tensor_copy(out=pd_[:, 3:3 + N], in_=pst[:, 0:N]), [pst.b()], [pd_.b()])
                            V(lambda: nc.vector.tensor_copy(out=ctail[:, jb, :], in_=pd_[:, N:N + 3]), [pd_.b()], [ctail.b(jb)])
                            V(lambda: nc.vector.tensor_scalar(out=ac[:, 0:N], in0=pd_[:, 0:N], scalar1=wconv[:, jb, 0:1],
                                                              scalar2=bconv[:, jb:jb + 1], op0=ALU.mult, op1=ALU.add),
                              [pd_.b(), wconv.b(), bconv.b()], [ac.b()])
                            for k in (1, 2, 3):
                                V(lambda: nc.vector.scalar_tensor_tensor(out=ac[:, 0:N], in0=pd_[:, k:k + N], scalar=wconv[:, jb, k:k + 1],
                                                                         in1=ac[:, 0:N], op0=ALU.mult, op1=ALU.add),
                                  [pd_.b(), ac.b()], [ac.b()])
                            A(lambda: nc.scalar.activation(out=xbc[:, jb, 0:N], in_=ac[:, 0:N], func=AF.Silu), [ac.b()], [xbc.b(jb)])
                        elif j == c_.NZ + c_.NXB:
                            H = c_.HEADS
                            A(lambda: nc.scalar.activation(out=dtT[:, 0:N], in_=pst[0:H, 0:N], func=AF.Exp, bias=dtb[:, 0:1]),
                              [pst.b(), dtb.b()], [dtT.b()])
                            A(lambda: nc.scalar.activation(out=dtT[:, 0:N], in_=dtT[:, 0:N], func=AF.Ln, bias=1.0), [dtT.b()], [dtT.b()])
                            V(lambda: nc.vector.tensor_scalar(out=laT[:, 0:N], in0=dtT[:, 0:N], scalar1=aneg[:, 0:1], scalar2=None, op0=ALU.mult),
                              [dtT.b(), aneg.b()], [laT.b()])
                        else:
                            jq = j - (c_.NZ + c_.NXB + 1)
                            V(lambda: nc.vector.tensor_copy(out=qm[:, jq, 0:N], in_=pst[:, 0:N]), [pst.b()], [qm.b(jq)])

                    Ms = [128] * (c_.NZ + c_.NXB) + [c_.HEADS] + [128] * c_.NQM
                    gemm(I["w_in_a"].t, I["w_in_a"].b(), NBA, KT, lambda kt: hT[:, kt, 0:N], lambda kt: hT.b(kt), N, pg, evacA, Ms)
                    S.barrier()
                with ExitStack() as st2:
                    H = c_.HEADS
                    dtk = sb("sdtk", [128, H], st=st2)
                    lak = sb("slak", [128, H], st=st2)
                    cum = sb("scum", [128, H], st=st2)
                    wend = sb("swend", [128, H], st=st2)
                    ecl = sb("secl", [128, H], st=st2)
                    xdt = sb("sxdt", [128, c_.INNER], BF16, st=st2)
                    xdtw = sb("sxdtw", [128, c_.INNER], BF16, st=st2)
                    Btok = sb("sBtok", [128, c_.G, 128], BF16, st=st2)
                    cbm = sb("scbm", [128, 128], st=st2)
                    seg = [sb("sseg%d" % i, [128, 128], st=st2) for i in range(2)]
                    LT = [sb("sLT%d" % i, [128, 128], BF16, st=st2) for i in range(2)]
                    ecum = [sb("secum%d" % i, [128, 128], st=st2) for i in range(2)]
                    Cs = [sb("sCs%d" % i, [128, 128], BF16, st=st2) for i in range(2)]
                    ytmp = sb("sytmp", [128, 128], st=st2)
                    stmp = sb("sstmp", [128, 384], st=st2)
                    ptr = ps("sptr", [128, 1024], BF16, st=st2)
                    psm = ps("spsm", [128, 512], st=st2)
                    pcb = ps("spcb", [128, 512], st=st2)
                    pcum = [ps("spcum%d" % i, [128, 512], st=st2) for i in range(2)]
                    py = [ps("spy%d" % i, [128, 512], st=st2) for i in range(2)]
                    pds = ps("spds", [128, 512], st=st2)
                    for ci in range(nch):
                        cs = slice(ci * c, (ci + 1) * c)
                        PE(lambda: nc.tensor.transpose(psm[0:c, 0:H], dtT[:, cs], ident[0:H, 0:H]), [dtT.b()] + CB, [psm.b()], inc=False)
                        PE(lambda: nc.tensor.transpose(psm[0:c, 64:64 + H], laT[:, cs], ident[0:H, 0:H]), [laT.b()] + CB, [psm.b()])
                        V(lambda: nc.vector.tensor_copy(out=dtk[0:c, :], in_=psm[0:c, 0:H]), [psm.b()], [dtk.b()])
                        V(lambda: nc.vector.tensor_copy(out=lak[0:c, :], in_=psm[0:c, 64:64 + H]), [psm.b()], [lak.b()])
                        PE(lambda: nc.tensor.matmul(psm[0:c, 128:128 + H], lhsT=Um[0:c, 0:c], rhs=lak[0:c, :], start=True, stop=True),
                           [lak.b()] + CB, [psm.b()], inc=False)
                        PE(lambda: nc.tensor.matmul(psm[:, 192:192 + H], lhsT=ones[0:c, :], rhs=lak[0:c, :], start=True, stop=True),
                           [lak.b()] + CB, [psm.b()])
                        V(lambda: nc.vector.tensor_copy(out=cum[0:c, :], in_=psm[0:c, 128:128 + H]), [psm.b()], [cum.b()])
                        V(lambda: nc.vector.tensor_tensor(out=wend[0:c, :], in0=psm[0:c, 192:192 + H], in1=cum[0:c, :], op=ALU.subtract),
                          [psm.b(), cum.b()], [wend.b()])
                        A(lambda: nc.scalar.activation(out=wend[0:c, :], in_=wend[0:c, :], func=AF.Exp), [wend.b()], [wend.b()])
                        A(lambda: nc.scalar.activation(out=ecl[:], in_=psm[:, 192:192 + H], func=AF.Exp), [psm.b()], [ecl.b()])
                        for j4 in range(0, c_.NZ, 4):
                            nb4 = min(4, c_.NZ - j4)
                            for q in range(nb4):
                                PE(lambda: nc.tensor.transpose(ptr[0:c, q * 128:(q + 1) * 128], xbc[:, j4 + q, cs], identb),
                                   [xbc.b(j4 + q)] + CB, [ptr.b()], inc=(q == nb4 - 1))
                            hs = slice(2 * j4, 2 * j4 + 2 * nb4)
                            fs = slice(j4 * 128, (j4 + nb4) * 128)
                            V(lambda: nc.vector.tensor_tensor(
                                out=xdt[0:c, fs].rearrange("p (h d) -> p h d", d=64),
                                in0=ptr[0:c, 0:nb4 * 128].rearrange("p (h d) -> p h d", d=64),
                                in1=dtk[0:c, hs].unsqueeze(2).to_broadcast([c, 2 * nb4, 64]), op=ALU.mult),
                              [ptr.b(), dtk.b()], [xdt.b()])
                            V(lambda: nc.vector.tensor_tensor(
                                out=xdtw[0:c, fs].rearrange("p (h d) -> p h d", d=64),
                                in0=xdt[0:c, fs].rearrange("p (h d) -> p h d", d=64),
                                in1=wend[0:c, hs].unsqueeze(2).to_broadcast([c, 2 * nb4, 64]), op=ALU.mult),
                              [xdt.b(), wend.b()], [xdtw.b()])
                        for g4 in range(0, c_.G, 4):
                            nb4 = min(4, c_.G - g4)
                            for q in range(nb4):
                                PE(lambda: nc.tensor.transpose(ptr[0:c, q * 128:(q + 1) * 128], xbc[:, c_.NZ + g4 + q, cs], identb),
                                   [xbc.b(c_.NZ + g4 + q)] + CB, [ptr.b()], inc=(q == nb4 - 1))
                            V(lambda: nc.vector.tensor_copy(out=Btok[0:c, g4:g4 + nb4, :],
                                                            in_=ptr[0:c, 0:nb4 * 128].rearrange("p (g d) -> p g d", d=128)),
                              [ptr.b()], [Btok.b()])
                        for g in range(c_.G):
                            BTg = xbc[:, c_.NZ + g, cs]
                            CTg = xbc[:, c_.NZ + c_.G + g, cs]
                            PE(lambda: nc.tensor.matmul(pcb[0:c, 0:c], lhsT=BTg, rhs=CTg, start=True, stop=True),
                               [xbc.b(c_.NZ + g), xbc.b(c_.NZ + c_.G + g)], [pcb.b()])
                            V(lambda: nc.vector.tensor_tensor(out=cbm[0:c, 0:c], in0=pcb[0:c, 0:c], in1=Um[0:c, 0:c], op=ALU.mult),
                              [pcb.b()] + CB, [cbm.b()])
                            for hh in range(6):
                                h = g * 6 + hh
                                i = h % 2
                                pc = pcum[i]
                                PE(lambda: nc.tensor.matmul(pc[:, 0:c], lhsT=lak[0:c, h:h + 1].to_broadcast([c, 128]), rhs=Um[0:c, 0:c],
                                                            start=True, stop=True), [lak.b()] + CB, [pc.b()])
                                V(lambda: nc.vector.tensor_scalar(out=seg[i][0:c, 0:c], in0=pc[0:c, 0:c], scalar1=cum[0:c, h:h + 1], scalar2=0.0,
                                                                  op0=ALU.subtract, op1=ALU.min), [pc.b(), cum.b()], [seg[i].b()])
                                A(lambda: nc.scalar.activation(out=seg[i][0:c, 0:c], in_=seg[i][0:c, 0:c], func=AF.Exp), [seg[i].b()], [seg[i].b()])
                                V(lambda: nc.vector.tensor_tensor(out=LT[i][0:c, 0:c], in0=seg[i][0:c, 0:c], in1=cbm[0:c, 0:c], op=ALU.mult),
                                  [seg[i].b(), cbm.b()], [LT[i].b()])
                                A(lambda: nc.scalar.activation(out=ecum[i][:, 0:c], in_=pc[:, 0:c], func=AF.Exp), [pc.b()], [ecum[i].b()])
                                V(lambda: nc.vector.tensor_tensor(out=Cs[i][:, 0:c], in0=CTg, in1=ecum[i][:, 0:c], op=ALU.mult),
                                  [xbc.b(c_.NZ + c_.G + g), ecum[i].b()], [Cs[i].b()])
                                jb = h // 2
                                pyy = py[jb % 2]
                                po = (h % 2) * 64
                                PE(lambda: nc.tensor.matmul(pyy[po:po + 64, 0:c], lhsT=xdt[0:c, h * 64:(h + 1) * 64], rhs=LT[i][0:c, 0:c],
                                                            start=True, stop=False), [xdt.b(), LT[i].b()], [pyy.b()], inc=False)
                                PE(lambda: nc.tensor.matmul(pyy[po:po + 64, 0:c], lhsT=Sbf[:, h * 64:(h + 1) * 64], rhs=Cs[i][:, 0:c],
                                                            start=False, stop=True), [Sbf.b(), Cs[i].b()], [pyy.b()])
                                if h % 2 == 1:
                                    V(lambda: nc.vector.scalar_tensor_tensor(out=ytmp[:, 0:c], in0=xbc[:, jb, cs], scalar=dsk[:, jb:jb + 1],
                                                                             in1=pyy[:, 0:c], op0=ALU.mult, op1=ALU.add),
                                      [xbc.b(jb), dsk.b(), pyy.b()], [ytmp.b()])
                                    V(lambda: nc.vector.tensor_tensor(out=sz[:, jb, cs], in0=ytmp[:, 0:c], in1=sz[:, jb, cs], op=ALU.mult),
                                      [ytmp.b(), sz.b(jb)], [sz.b(jb)])
                        for g in range(c_.G):
                            gs_ = slice(g * 384, (g + 1) * 384)
                            PE(lambda: nc.tensor.matmul(pds[:, 0:384], lhsT=Btok[0:c, g, :], rhs=xdtw[0:c, gs_], start=True, stop=True),
                               [Btok.b(), xdtw.b()], [pds.b()])
                            V(lambda: nc.vector.tensor_tensor(
                                out=stmp[:].rearrange("p (h d) -> p h d", d=64), in0=Sst[:, gs_].rearrange("p (h d) -> p h d", d=64),
                                in1=ecl[:, g * 6:(g + 1) * 6].unsqueeze(2).to_broadcast([128, 6, 64]), op=ALU.mult),
                              [Sst.b(), ecl.b()], [stmp.b()])
                            V(lambda: nc.vector.tensor_tensor(out=Sst[:, gs_], in0=stmp[:], in1=pds[:, 0:384], op=ALU.add),
                              [stmp.b(), pds.b(), Sbf.b()], [Sst.b()])
                        V(lambda: nc.vector.tensor_copy(out=Sbf[:], in_=Sst[:]), [Sst.b()], [Sbf.b()])
                    S.barrier()
                with ExitStack() as st2:
                    sq = [sb("gsq%d" % i, [128, NT], BF16, st=st2) for i in range(2)]
                    rs = sb("grs", [128, NT], st=st2)
                    pn = ps("gpn", [128, 512], st=st2)
                    for g in range(c_.G):
                        for b3 in range(3):
                            j = g * 3 + b3
                            s_ = sq[b3 % 2]
                            A(lambda: nc.scalar.activation(out=s_[:, 0:N], in_=sz[:, j, 0:N], func=AF.Square), [sz.b(j)], [s_.b()])
                            PE(lambda: nc.tensor.matmul(pn[:, 0:N], lhsT=onesb, rhs=s_[:, 0:N], start=(b3 == 0), stop=(b3 == 2)),
                               [s_.b()] + CB, [pn.b()])
                        rstd_from(pn, rs, N, 384)
                        for b3 in range(3):
                            j = g * 3 + b3
                            V(lambda: nc.vector.scalar_tensor_tensor(out=sz[:, j, 0:N], in0=sz[:, j, 0:N], scalar=gout[:, j:j + 1], in1=rs[:, 0:N],
                                                                     op0=ALU.mult, op1=ALU.mult), [sz.b(j), rs.b(), gout.b()], [sz.b(j)])
                    S.barrier()
                with ExitStack() as st2:
                    mem_attn(0, qm, ymem, st2)
                    S.barrier()
                with ExitStack() as st2:
                    xin = [sb("axin%d" % i, [128, NT], st=st2) for i in range(2)]
                    xo = [sb("axo%d" % i, [128, NT], st=st2) for i in range(2)]
                    pg = [ps("aopg%d" % i, [128, 512], st=st2) for i in range(2)]
                    mv = lambda kt: (sz[:, kt, 0:N] if kt < c_.NZ else ymem[:, kt - c_.NZ, 0:N])
                    mvb = lambda kt: (sz.b(kt) if kt < c_.NZ else ymem.b(kt - c_.NZ))
                    gemm(I["w_out_a"].t, I["w_out_a"].b(), KT, c_.A_MIX // 128, mv, mvb, N, pg, make_resid_evac(N, xin, xo))
                    S.barrier()
            ffn(0)
            with ExitStack() as st:
                hT = sb("khT", [128, KT, NT], BF16, st=st)
                norm_phase(gkv, hT, st)
                wk = [sb("kwk%d" % i, [128, NT], st=st) for i in range(4)]
                vraw = sb("kvraw", [128, NT], st=st)
                stg = [sb("kstg%d" % i, [128, NT // 128 if c == 128 else 1, 128], st=st) for i in range(2)]
                pg = [ps("kpg%d" % i, [128, 512], st=st) for i in range(3)]
                pn = ps("kpn", [128, 512], st=st)
                pr = ps("kpr", [128, 512], st=st)
                pt = [ps("kpt%d" % i, [128, 512], st=st) for i in range(2)]
                kc = [0]

                def evacK(j, pst, M):
                    g, kv = j // 16, (j // 8) % 2
                    i = kc[0] % 2
                    kc[0] += 1
                    if kv == 0:
                        normrope(pst, gk[:, g:g + 1], vraw[:, 0:N], vraw.b(), wk, pn, pr)
                    else:
                        V(lambda: nc.vector.tensor_copy(out=vraw[:, 0:N], in_=pst[:, 0:N]), [pst.b()], [vraw.b()])
                    p = pt[i]
                    for ci in range(nch):
                        PE(lambda: nc.tensor.transpose(p[0:c, ci * 128:(ci + 1) * 128], vraw[:, ci * c:(ci + 1) * c], ident),
                           [vraw.b()] + CB, [p.b()], inc=(ci == nch - 1))
                    V(lambda: nc.vector.tensor_copy(out=stg[i][0:c, :, :], in_=p[0:c, 0:nch * 128].rearrange("p (a m) -> p a m", m=128)),
                      [p.b()], [stg[i].b()])
                    S.dma("pool", kv_dst.t[t0:t0 + N, j * 128:(j + 1) * 128].rearrange("(a p) m -> p a m", p=c), stg[i][0:c, :, :],
                          r=[stg[i].b()], w=[kv_dst.b()])

                gemm(I["w_kv"].t, I["w_kv"].b(), 48, KT, lambda kt: hT[:, kt, 0:N], lambda kt: hT.b(kt), N, pg, evacK)
                S.barrier()
            with ExitStack() as st:
                QT = sb("bQT", [128, 24, NT], BF16, st=st)
                qm = sb("bqm", [128, c_.NQM, NT], st=st)
                ymem = sb("bymem", [128, c_.NQM, NT], BF16, st=st)
                ydil = sb("bydil", [128, 8, NT], BF16, st=st)
                with ExitStack() as st2:
                    hT = sb("bhT", [128, KT, NT], BF16, st=st2)
                    norm_phase(gmix[1], hT, st2)
                    wk = [sb("bwk%d" % i, [128, NT], st=st2) for i in range(4)]
                    pg = [ps("bpg%d" % i, [128, 512], st=st2) for i in range(3)]
                    pn = ps("bpn", [128, 512], st=st2)
                    pr = ps("bpr", [128, 512], st=st2)

                    def evacB(j, pst, M):
                        if j < 24:
                            normrope(pst, gq[:, j // 8:j // 8 + 1], QT[:, j, 0:N], QT.b(j), wk, pn, pr)
                        else:
                            V(lambda: nc.vector.tensor_copy(out=qm[:, j - 24, 0:N], in_=pst[:, 0:N]), [pst.b()], [qm.b(j - 24)])
                    gemm(I["w_in_b"].t, I["w_in_b"].b(), 24 + c_.NQM, KT, lambda kt: hT[:, kt, 0:N], lambda kt: hT.b(kt), N, pg, evacB)
                    S.barrier()
                with ExitStack() as st2:
                    Oacc = sb("dOacc", [128, 8, NT], st=st2)
                    Dacc = sb("dDacc", [128, 8, NT], st=st2)
                    kvld = [sb("dkvld%d" % i, [128, 2048], st=st2) for i in range(2)]
                    Vbf = [sb("dVbf%d" % i, [128, 1024], BF16, st=st2) for i in range(2)]
                    KTb = [sb("dKTb%d" % i, [128, 8, 128], BF16, st=st2) for i in range(2)]
                    Pf = [sb("dPf%d" % i, [128, 128], st=st2) for i in range(2)]
                    PT = [sb("dPT%d" % i, [128, 128], BF16, st=st2) for i in range(2)]
                    ptk = [ps("dptk%d" % i, [128, 512], st=st2) for i in range(2)]
                    pS = [ps("dpS%d" % i, [128, 512], st=st2) for i in range(2)]
                    pO = [ps("dpO%d" % i, [128, 512], st=st2) for i in range(2)]
                    pD = [ps("dpD%d" % i, [128, 512], st=st2) for i in range(2)]
                    scale = 128 ** -0.5
                    units = []
                    if sample:
                        for g, (W, r) in enumerate(DIL):
                            segs = [(I["c_win%d" % g], 0, r, 128, Lm, 0, 0), (kv_dst, 0, 1, 1, Um, 0, g * 2048)]
                            units.append((g, slice(0, 1), 1, segs))
                    else:
                        for g, (W, r) in enumerate(DIL):
                            npc = N // r
                            j0 = t0 // r
                            for rho in range(r):
                                for ja in range(j0, j0 + npc, 128):
                                    nq = min(128, j0 + npc - ja)
                                    Bk, off = ja // 128, ja % 128
                                    qs = slice(rho + r * (ja - j0), rho + r * (ja - j0) + r * (nq - 1) + 1, r)
                                    segs = []
                                    if Bk >= 1:
                                        segs.append((kv_dst, rho + r * 128 * (Bk - 1), r, 128, Lm, off, g * 2048))
                                    segs.append((kv_dst, rho + r * 128 * Bk, r, off + nq, Um, off, g * 2048))
                                    units.append((g, qs, nq, segs))
                    first = [True] * 1
                    seen_g0 = set()
                    uc = [0]
                    for (g, qs, nq, segs) in units:
                        for si, (src, row0, rstep, nk, msk, moff, col0) in enumerate(segs):
                            kl = kvld[si]
                            S.dma("sp", kl[0:nk, :], src.t[row0:row0 + rstep * (nk - 1) + 1:rstep, col0:col0 + 2048], r=[src.b()], w=[kl.b()])
                            A(lambda: nc.scalar.activation(out=Vbf[si][0:nk, :], in_=kl[0:nk, 1024:2048], func=AF.Copy), [kl.b()], [Vbf[si].b()])
                            for h4 in range(0, 8, 4):
                                p = ptk[(h4 // 4) % 2]
                                for q in range(4):
                                    PE(lambda: nc.tensor.transpose(p[:, q * 128:q * 128 + nk], kl[0:nk, (h4 + q) * 128:(h4 + q + 1) * 128], ident[0:nk, 0:nk]),
                                       [kl.b()] + CB, [p.b()], inc=(q == 3))
                                V(lambda: nc.vector.tensor_copy(out=KTb[si][:, h4:h4 + 4, 0:nk],
                                                                in_=p[:].rearrange("p (q m) -> p q m", q=4)[:, :, 0:nk]),
                                  [p.b()], [KTb[si].b()])
                        for h in range(8):
                            hb, hq = h // 4, h % 4
                            pts = []
                            for si, (src, row0, rstep, nk, msk, moff, col0) in enumerate(segs):
                                i = (uc[0]) % 2
                                uc[0] += 1
                                p = pS[i]
                                PE(lambda: nc.tensor.matmul(p[0:nk, 0:nq], lhsT=KTb[si][:, h, 0:nk], rhs=QT[:, g * 8 + h, qs], start=True, stop=True),
                                   [KTb[si].b(), QT.b(g * 8 + h)], [p.b()])
                                A(lambda: nc.scalar.activation(out=Pf[i][0:nk, 0:nq], in_=p[0:nk, 0:nq], func=AF.Exp, scale=scale), [p.b()], [Pf[i].b()])
                                V(lambda: nc.vector.tensor_tensor(out=PT[i][0:nk, 0:nq], in0=Pf[i][0:nk, 0:nq], in1=msk[0:nk, moff:moff + nq], op=ALU.mult),
                                  [Pf[i].b()] + CB, [PT[i].b()])
                                pts.append((i, nk, si))
                            for k_, (i, nk, si) in enumerate(pts):
                                PE(lambda: nc.tensor.matmul(pO[hb][:, hq * 128:hq * 128 + nq], lhsT=Vbf[si][0:nk, h * 128:(h + 1) * 128], rhs=PT[i][0:nk, 0:nq],
                                                            start=(k_ == 0), stop=(k_ == len(pts) - 1)), [Vbf[si].b(), PT[i].b()], [pO[hb].b()],
                                   inc=(k_ == len(pts) - 1))
                            for k_, (i, nk, si) in enumerate(pts):
                                PE(lambda: nc.tensor.matmul(pD[hb][:, hq * 128:hq * 128 + nq], lhsT=onesb[0:nk, :], rhs=PT[i][0:nk, 0:nq],
                                                            start=(k_ == 0), stop=(k_ == len(pts) - 1)), [PT[i].b()] + CB, [pD[hb].b()],
                                   inc=(k_ == len(pts) - 1))
                        for hb in range(2):
                            for (acc_, pp_) in ((Oacc, pO[hb]), (Dacc, pD[hb])):
                                src_ = pp_[:].rearrange("p (q m) -> p q m", q=4)[:, :, 0:nq]
                                dst_ = acc_[:, hb * 4:(hb + 1) * 4, qs]
                                if g == 0:
                                    V(lambda: nc.vector.tensor_copy(out=dst_, in_=src_), [pp_.b()], [acc_.b()])
                                else:
                                    V(lambda: nc.vector.tensor_tensor(out=dst_, in0=src_, in1=dst_, op=ALU.add), [pp_.b(), acc_.b()], [acc_.b()])
                    V(lambda: nc.vector.reciprocal(out=Dacc[:, :, 0:N], in_=Dacc[:, :, 0:N]), [Dacc.b()], [Dacc.b()])
                    V(lambda: nc.vector.tensor_tensor(out=ydil[:, :, 0:N], in0=Oacc[:, :, 0:N], in1=Dacc[:, :, 0:N], op=ALU.mult),
                      [Oacc.b(), Dacc.b()], ydil.bs(range(8)))
                    S.barrier()
                with ExitStack() as st2:
                    mem_attn(1, qm, ymem, st2)
                    S.barrier()
                with ExitStack() as st2:
                    xin = [sb("bxin%d" % i, [128, NT], st=st2) for i in range(2)]
                    xo = [sb("bxo%d" % i, [128, NT], st=st2) for i in range(2)]
                    pg = [ps("bopg%d" % i, [128, 512], st=st2) for i in range(2)]
                    mv = lambda kt: (ydil[:, kt, 0:N] if kt < 8 else ymem[:, kt - 8, 0:N])
                    mvb = lambda kt: (ydil.b(kt) if kt < 8 else ymem.b(kt - 8))
                    gemm(I["w_out_b"].t, I["w_out_b"].b(), KT, c_.B_MIX // 128, mv, mvb, N, pg, make_resid_evac(N, xin, xo))
                    S.barrier()
            ffn(1)
            with ExitStack() as st:
                XB = sb("oXB", [128, KT, NT], st=st)
                ytok = [sb("oytok%d" % i, [128, D], st=st) for i in range(2)]
                ptr = [ps("optr%d" % i, [128, 512], st=st) for i in range(2)]
                load_xres(XB, N)
                for ci in range(nch):
                    yt = ytok[ci % 2]
                    for k4 in range(0, KT, 4):
                        p = ptr[(k4 // 4) % 2]
                        for q in range(4):
                            PE(lambda: nc.tensor.transpose(p[0:c, q * 128:(q + 1) * 128], XB[:, k4 + q, ci * c:(ci + 1) * c], ident),
                               [XB.b(k4 + q)] + CB, [p.b()], inc=(q == 3))
                        V(lambda: nc.vector.tensor_copy(out=yt[0:c, k4 * 128:(k4 + 4) * 128], in_=p[0:c, :]), [p.b()], [yt.b()])
                    S.dma("pool", y_dst.t[ci * c:(ci + 1) * c, :], yt[0:c, :], r=[yt.b()], w=[y_dst.b()])
                S.barrier()

        def zero_state():
            V(lambda: nc.vector.memset(Sst[:], 0.0), [], [Sst.b()])
            V(lambda: nc.vector.memset(Sbf[:], 0.0), [], [Sbf.b()])
            V(lambda: nc.vector.memset(ctail[:], 0.0), [], ctail.bs(range(c_.NXB)))
            for i in range(2):
                V(lambda: nc.vector.memset(ftail[i][:], 0.0), [], ftail[i].bs(range(c_.NF)))

        def load_state():
            with ExitStack() as st:
                ld_ = sb("lst", [128, 128], st=st)
                rows = sb("lrows", [3, max(c_.XBC, c_.DFF)], st=st)
                pt = ps("lpt", [128, 512], st=st)
                for j in range(c_.NZ):
                    S.dma("sp", ld_[:], I["st_ssm"].t[j * 128:(j + 1) * 128, :], w=[ld_.b()])
                    PE(lambda: nc.tensor.transpose(pt[:, 0:128], ld_[:], ident), [ld_.b()] + CB, [pt.b()])
                    V(lambda: nc.vector.tensor_copy(out=Sst[:, j * 128:(j + 1) * 128], in_=pt[:, 0:128]), [pt.b()], [Sst.b()])
                V(lambda: nc.vector.tensor_copy(out=Sbf[:], in_=Sst[:]), [Sst.b()], [Sbf.b()])
                S.dma("sp", rows[0:3, 0:c_.XBC], I["st_conv"].t, w=[rows.b()])
                for j in range(c_.NXB):
                    PE(lambda: nc.tensor.transpose(pt[:, 0:3], rows[0:3, j * 128:(j + 1) * 128], ident[0:3, 0:3]), [rows.b()] + CB, [pt.b()])
                    V(lambda: nc.vector.tensor_copy(out=ctail[:, j, :], in_=pt[:, 0:3]), [pt.b()], [ctail.b(j)])
                for li in range(2):
                    S.dma("sp", rows[0:2, 0:c_.DFF], I["st_ffn"].t[li], w=[rows.b()])
                    for j in range(c_.NF):
                        PE(lambda: nc.tensor.transpose(pt[:, 0:2], rows[0:2, j * 128:(j + 1) * 128], ident[0:2, 0:2]), [rows.b()] + CB, [pt.b()])
                        V(lambda: nc.vector.tensor_copy(out=ftail[li][:, j, :], in_=pt[:, 0:2]), [pt.b()], [ftail[li].b(j)])
                S.barrier()

        def store_state(o_ssm, o_conv, o_ffn):
            with ExitStack() as st:
                stg = [sb("sst%d" % i, [128, 128], st=st) for i in range(2)]
                rows = sb("srows", [3, max(c_.XBC, c_.DFF)], st=st)
                pt = ps("sspt", [128, 512], st=st)
                for j in range(c_.NZ):
                    PE(lambda: nc.tensor.transpose(pt[:, 0:128], Sst[:, j * 128:(j + 1) * 128], ident), [Sst.b()] + CB, [pt.b()])
                    V(lambda: nc.vector.tensor_copy(out=stg[j % 2][:], in_=pt[:, 0:128]), [pt.b()], [stg[j % 2].b()])
                    S.dma("pool", o_ssm.t[j * 128:(j + 1) * 128, :], stg[j % 2][:], r=[stg[j % 2].b()], w=[o_ssm.b()])
                for j in range(c_.NXB):
                    PE(lambda: nc.tensor.transpose(pt[0:3, 0:128], ctail[:, j, :], ident), [ctail.b(j)] + CB, [pt.b()])
                    V(lambda: nc.vector.tensor_copy(out=rows[0:3, j * 128:(j + 1) * 128], in_=pt[0:3, 0:128]), [pt.b()], [rows.b()])
                S.dma("pool", o_conv.t, rows[0:3, 0:c_.XBC], r=[rows.b()], w=[o_conv.b()])
                for li in range(2):
                    for j in range(c_.NF):
                        PE(lambda: nc.tensor.transpose(pt[0:2, 0:128], ftail[li][:, j, :], ident), [ftail[li].b(j)] + CB, [pt.b()])
                        V(lambda: nc.vector.tensor_copy(out=rows[0:2, j * 128:(j + 1) * 128], in_=pt[0:2, 0:128]), [pt.b()], [rows.b()])
                    S.dma("pool", o_ffn.t[li], rows[0:2, 0:c_.DFF], r=[rows.b()], w=[o_ffn.b(li)])
                S.barrier()

        S.barrier()
        if do_prompt:
            mem_kv_prompt()
            if dbg_stop >= 2:
                zero_state()
                for ti in range(NTILES if dbg_stop >= 3 else 1):
                    xs_ = T(I["xp"].t[ti * NT:(ti + 1) * NT, :])
                    ys_ = T(O["y_p"].t[ti * NT:(ti + 1) * NT, :])
                    trunk(ti, NT, 128, xs_, ys_, X["kvs"], ti * NT, O["memkv_p"], ti == NTILES - 1, False)
            if dbg_stop >= 4:
                store_state(O["ssm_p"], O["conv_p"], O["ffn_p"])
            if dbg_stop >= 5:
                for g in range(3):
                    L = c_.WL[g]
                    S.dma("sp", O["win%d_p" % g].t, X["kvs"].t[SEQ - L:SEQ, g * 2048:(g + 1) * 2048], r=[X["kvs"].b()], w=[O["win%d_p" % g].b()])
        if do_sample:
            load_state()
            trunk(NTILES, 1, 1, I["xs"], O["y_s"], X["kvs_s"], 0, I["c_mem"], True, True)
            store_state(O["ssm_s"], O["conv_s"], O["ffn_s"])
            for g in range(3):
                S.dma("sp", O["win%d_s" % g].t, X["kvs_s"].t[0:1, g * 2048:(g + 1) * 2048], r=[X["kvs_s"].b()], w=[O["win%d_s" % g].b()])
        for E in S.ENG:
            S.flush(E)
        S.barrier()
    return nc


def prep_shared(cfg, inp):
    c_ = cfg
    KT = c_.KT
    f = lambda a: np.ascontiguousarray(np.asarray(a, np.float32))
    sh = {}
    colsA = [(j * 128, 128) for j in range(c_.NZ + c_.NXB)] + [(c_.INNER + c_.XBC, c_.HEADS)] + \
            [(c_.INNER + c_.XBC + c_.HEADS + j * 128, 128) for j in range(c_.NQM)]
    full = lambda n: [(j * 128, 128) for j in range(n)]
    sh["w_in_a"] = blockify(f(inp["w_in_a"][0]), colsA)
    sh["w_out_a"] = blockify(f(inp["w_out_a"][0]), full(KT))
    sh["w_kv"] = blockify(f(inp["w_kv"]), full(48))
    sh["w_in_b"] = blockify(f(inp["w_in_b"][0]), full(24 + c_.NQM))
    sh["w_out_b"] = blockify(f(inp["w_out_b"][0]), full(KT))
    sh["w_mem_kv"] = np.stack([blockify(f(inp["w_mem_kv"][i]), full(2 * c_.NQM)) for i in range(2)])
    upcols = []
    for j in range(c_.NF):
        upcols += [(j * 128, 128), (c_.DFF + j * 128, 128)]
    sh["w_up"] = np.stack([blockify(f(inp["w_ffn_up"][i]), upcols) for i in range(2)])
    sh["w_down"] = np.stack([blockify(f(inp["w_ffn_down"][i]), full(KT)) for i in range(2)])
    sh["g_mix"] = np.stack([pp(inp["g_mix"][i]) for i in range(2)])
    sh["g_ffn"] = np.stack([pp(inp["g_ffn"][i]) for i in range(2)])
    sh["g_kv"] = pp(inp["g_kv"])
    sh["g_mem"] = np.stack([np.tile(f(inp["g_mem"][i])[None, :], (128, 1)) for i in range(2)])
    sh["g_mem_k"] = np.stack([np.tile(f(inp["g_mem_k"][i])[None, :], (128, 1)) for i in range(2)])
    sh["g_mem_q"] = np.stack([pp(inp["g_mem_q"][i]) for i in range(2)])
    sh["g_mem_kp"] = np.stack([pp(inp["g_mem_k"][i]) for i in range(2)])
    wc = f(inp["w_conv_a"][0])
    sh["wconv"] = np.ascontiguousarray(wc.reshape(4, c_.NXB, 128).transpose(2, 1, 0))
    sh["bconv"] = pp(inp["b_conv_a"][0])
    sh["dtb"] = f(inp["dt_bias_a"][0]).reshape(-1, 1)
    sh["alog"] = f(inp["a_log_a"][0]).reshape(-1, 1)
    sh["dsk"] = pp(np.repeat(f(inp["d_skip_a"][0]), 64))
    sh["gout"] = pp(inp["g_ssm_out_a"][0])
    sh["gk"] = np.ascontiguousarray(f(inp["g_k_dil"]).T)
    sh["gq"] = np.ascontiguousarray(f(inp["g_q_dil"][0]).T)
    sh["wfc"] = np.stack([np.ascontiguousarray(f(inp["w_ffn_conv"][i]).reshape(3, c_.NF, 128).transpose(2, 1, 0)) for i in range(2)])
    sh["bfc"] = np.stack([pp(inp["b_ffn_conv"][i]) for i in range(2)])
    NTILES = c_.SEQ // c_.NT
    cosf = np.ones((NTILES + 1, 128, c_.NT), np.float32)
    sinf = np.zeros((NTILES + 1, 128, c_.NT), np.float32)
    for ti in range(NTILES):
        cosf[ti], sinf[ti] = rope_tables(np.arange(ti * c_.NT, (ti + 1) * c_.NT, dtype=np.int32))
    cs, sn = rope_tables(np.array([c_.PAST], dtype=np.int32))
    cosf[NTILES, :, 0:1], sinf[NTILES, :, 0:1] = cs, sn
    sh["cosf"], sh["sinf"] = cosf, sinf
    k = np.arange(128)
    ident = np.eye(128, dtype=np.float32)
    ones = np.ones((128, 128), np.float32)
    U = (k[:, None] <= k[None, :]).astype(np.float32)
    L = (k[:, None] >= k[None, :]).astype(np.float32)
    R = np.zeros((128, 128), np.float32)
    for i in range(16):
        R[i, i + 16] = -1.0
        R[i + 16, i] = 1.0
    sh["consts"] = np.stack([ident, ones, U, L, np.ascontiguousarray(R.T)])
    return sh


def core_inputs(cfg, inp, sh, bp, bs):
    c_ = cfg
    f = lambda a: np.ascontiguousarray(np.asarray(a, np.float32))
    m = dict(sh)
    m["xp"] = f(inp["x_prompt"][bp])
    m["xs"] = f(inp["x_sample"][bs])
    m["st_ssm"] = f(inp["state_ssm"][0, bs]).reshape(c_.INNER, 128)
    m["st_conv"] = f(inp["state_ssm_conv"][0, bs])
    m["st_ffn"] = f(inp["state_ffn_conv"][:, bs])
    m["c_mem"] = f(inp["cache_mem_kv"][:, bs]).reshape(2, 256, 2 * c_.MEMW)
    for g in range(3):
        m["c_win%d" % g] = f(inp["cache_win_kv%d" % g][bs]).reshape(c_.CL[g], 2048)
    m["memp"] = f(inp["mem_prompt"][bp])
    return m


def assemble(cfg, res, nbp, nbs):
    c_ = cfg
    R = res
    st = lambda key, n: np.stack([R[i][key] for i in range(n)])
    y_p = st("y_p", nbp)
    y_s = st("y_s", nbs)
    ssm_p = st("ssm_p", nbp).reshape(1, nbp, c_.HEADS, 64, 128)
    ssm_s = st("ssm_s", nbs).reshape(1, nbs, c_.HEADS, 64, 128)
    conv_p = st("conv_p", nbp)[None]
    conv_s = st("conv_s", nbs)[None]
    ffn_p = np.ascontiguousarray(st("ffn_p", nbp).transpose(1, 0, 2, 3))
    ffn_s = np.ascontiguousarray(st("ffn_s", nbs).transpose(1, 0, 2, 3))
    memkv = np.ascontiguousarray(st("memkv_p", nbp).transpose(1, 0, 2, 3)).reshape(2, nbp, 256, 2, 4, c_.MHD)
    outs = [y_p, y_s, ssm_p, ssm_s, conv_p, conv_s, ffn_p, ffn_s, memkv]
    for g in range(3):
        outs.append(st("win%d_p" % g, nbp).reshape(nbp, c_.WL[g], 2, 8, 128))
    for g in range(3):
        outs.append(st("win%d_s" % g, nbs).reshape(nbs, 1, 2, 8, 128))
    return tuple(np.ascontiguousarray(o.astype(np.float32)) for o in outs)


def kernel(**inputs):
    cfg = Cfg()
    nc = build(cfg)
    sh = prep_shared(cfg, inputs)
    in_maps = [core_inputs(cfg, inputs, sh, i % 4, i) for i in range(8)]
    res = run_bass_kernel_spmd(nc, in_maps, core_ids=list(range(8)))
    return assemble(cfg, res.results, 4, 8)
```

```python
import math
from contextlib import ExitStack
import numpy as np
import concourse.bass as bass
import concourse.mybir as mybir
from concourse.bass_utils import run_bass_kernel_spmd

F32 = mybir.dt.float32
BF16 = mybir.dt.bfloat16
AF = mybir.ActivationFunctionType
ALU = mybir.AluOpType
EPS = 1e-6
DIL = ((128, 1), (512, 4), (2048, 16))
ROPE_THETA = 500000.0


class Cfg:
    def __init__(s, D=4096, G=8, DFF=11008, SEQ=2048, NT=512, PAST=16384):
        s.D, s.G, s.DFF, s.SEQ, s.NT, s.PAST = D, G, DFF, SEQ, NT, PAST
        s.KT = D // 128
        s.INNER = G * 384
        s.HEADS = s.INNER // 64
        s.XBC = s.INNER + 2 * G * 128
        s.MHD = 256
        s.MEMW = 1024
        s.NZ = s.INNER // 128
        s.NXB = s.XBC // 128
        s.NQM = s.MEMW // 128
        s.NF = DFF // 128
        s.A_IN = s.INNER + s.XBC + s.HEADS + s.MEMW
        s.A_MIX = s.INNER + s.MEMW
        s.B_MIX = 1024 + s.MEMW
        s.WL = [min(w, SEQ) for w, _ in DIL]
        s.CL = [min(w, PAST) for w, _ in DIL]


class Buf:
    __slots__ = ("w", "r", "ps")

    def __init__(s, ps=False):
        s.w = None
        s.r = {}
        s.ps = ps


class Sync:
    ENG = ("pe", "act", "dve", "pool", "sp")

    def __init__(s, nc, es, ndma=24):
        s.nc = nc
        s.e = {"pe": nc.tensor, "act": nc.scalar, "dve": nc.vector, "pool": nc.gpsimd, "sp": nc.sync}
        s.sem = {k: es.enter_context(nc.semaphore("s_" + k)) for k in s.ENG}
        s.cnt = {k: 0 for k in s.ENG}
        s.pend = {k: False for k in s.ENG}
        s.nd = ndma
        for i in range(ndma):
            s.sem["d%d" % i] = es.enter_context(nc.semaphore("s_d%d" % i))
            s.cnt["d%d" % i] = 0
        s.seen = {k: {} for k in s.ENG}
        s.rr = {"sp": 0, "pool": 0, "act": 0}

    def _wait(s, E, deps):
        need = {}
        for k, v in deps:
            if v > need.get(k, 0):
                need[k] = v
        for k, v in need.items():
            if k == E and E in ("pe", "sp"):
                continue
            if s.seen[E].get(k, 0) >= v:
                continue
            assert v <= s.cnt[k] + (1 if s.pend.get(k, False) else 0), (E, k, v, s.cnt[k])
            if s.pend.get(k, False) and v > s.cnt[k]:
                raise RuntimeError("wait on pending (non-incremented) op %s->%s" % (k, E))
            s.e[E].wait_ge(s.sem[k], v)
            s.seen[E][k] = v

    def _deps(s, r, w):
        deps = []
        for b in r:
            if b.w is not None:
                deps.append(b.w)
        for b in w:
            if b.w is not None:
                deps.append(b.w)
            deps.extend(b.r.items())
        return deps

    def _mark(s, ev, r, w):
        k, v = ev
        for b in r:
            if b.r.get(k, 0) < v:
                b.r[k] = v
        for b in w:
            b.w = ev
            b.r = {}

    def op(s, E, emit, r=(), w=(), inc=True):
        if any(b.ps for b in r):
            w = list(w) + [b for b in r if b.ps]
            r = [b for b in r if not b.ps]
        s._wait(E, s._deps(r, w))
        ins = emit()
        if inc:
            s.cnt[E] += 1
            ins.then_inc(s.sem[E], 1)
            s.pend[E] = False
            ev = (E, s.cnt[E])
        else:
            s.pend[E] = True
            ev = (E, s.cnt[E] + 1)
        s._mark(ev, r, w)
        return ins

    def dma(s, Q, out, in_, r=(), w=(), **kw):
        deps = s._deps(r, w)
        half = s.nd // 2
        base = half if Q == "pool" else 0
        d = "d%d" % (base + s.rr[Q] % half)
        s.rr[Q] += 1
        deps.append((d, s.cnt[d]))
        s._wait(Q, deps)
        s.cnt[d] += 16
        s.e[Q].dma_start(out=out, in_=in_, **kw).then_inc(s.sem[d], 16)
        s._mark((d, s.cnt[d]), r, w)

    def flush(s, E):
        if s.pend[E]:
            s.op(E, lambda: s.e[E].nop(), inc=True)

    def barrier(s):
        for E in s.ENG:
            assert not s.pend[E], E
        allev = [(k, v) for k, v in s.cnt.items() if v > 0]
        for E in s.ENG:
            s._wait(E, allev)


class T:
    def __init__(s, t, psum=False):
        s.t = t
        s.bufs = {}
        s.psum = psum

    def b(s, *key):
        if key not in s.bufs:
            s.bufs[key] = Buf(s.psum)
        return s.bufs[key]

    def bs(s, keys):
        return [s.b(k) for k in keys]

    def __getitem__(s, k):
        return s.t[k]


def blockify(W, blocks):
    K = W.shape[0]
    KT = K // 128
    out = np.zeros((len(blocks), 128, KT, 128), np.float32)
    for i, (c0, M) in enumerate(blocks):
        out[i, :, :, :M] = W[:, c0:c0 + M].reshape(KT, 128, M).transpose(1, 0, 2)
    return out


def pp(v):
    v = np.asarray(v, np.float32)
    return np.ascontiguousarray(v.reshape(-1, 128).T)


def rope_tables(pos):
    half = 16
    inv = np.exp(-(2.0 * np.arange(half, dtype=np.float32) / 32.0) * np.float32(math.log(ROPE_THETA))).astype(np.float32)
    ang = pos.astype(np.float32)[:, None] * inv[None, :]
    cos = np.cos(ang).astype(np.float32).T
    sin = np.sin(ang).astype(np.float32).T
    n = pos.shape[0]
    cf = np.ones((128, n), np.float32)
    sf = np.zeros((128, n), np.float32)
    cf[0:16] = cos
    cf[16:32] = cos
    sf[0:16] = sin
    sf[16:32] = sin
    return cf, sf


def build(cfg, do_prompt=True, do_sample=True, dbg_stop=99):
    c_ = cfg
    nc = bass.Bass("TRN2", target_bir_lowering=False)
    D, KT, NT, SEQ = c_.D, c_.KT, c_.NT, c_.SEQ
    NTILES = SEQ // NT
    NPASS = NTILES + 1

    def din(name, shape):
        return T(nc.dram_tensor(name, list(shape), F32, kind="ExternalInput").ap())

    def dout(name, shape):
        return T(nc.dram_tensor(name, list(shape), F32, kind="ExternalOutput").ap())

    def dscr(name, shape):
        return T(nc.dram_tensor(name, list(shape), F32, kind="Internal").ap())

    I = {}
    I["xp"] = din("xp", [SEQ, D])
    I["xs"] = din("xs", [1, D])
    I["st_ssm"] = din("st_ssm", [c_.INNER, 128])
    I["st_conv"] = din("st_conv", [3, c_.XBC])
    I["st_ffn"] = din("st_ffn", [2, 2, c_.DFF])
    I["c_mem"] = din("c_mem", [2, 256, 2 * c_.MEMW])
    for g in range(3):
        I["c_win%d" % g] = din("c_win%d" % g, [c_.CL[g], 2048])
    I["memp"] = din("memp", [256, D])
    colsA = [(j * 128, 128) for j in range(c_.NZ + c_.NXB)] + [(c_.INNER + c_.XBC, c_.HEADS)] + \
            [(c_.INNER + c_.XBC + c_.HEADS + j * 128, 128) for j in range(c_.NQM)]
    NBA = len(colsA)
    I["w_in_a"] = din("w_in_a", [NBA, 128, KT, 128])
    I["w_out_a"] = din("w_out_a", [KT, 128, c_.A_MIX // 128, 128])
    I["w_kv"] = din("w_kv", [48, 128, KT, 128])
    I["w_in_b"] = din("w_in_b", [24 + c_.NQM, 128, KT, 128])
    I["w_out_b"] = din("w_out_b", [KT, 128, c_.B_MIX // 128, 128])
    I["w_mem_kv"] = din("w_mem_kv", [2, 2 * c_.NQM, 128, KT, 128])
    I["w_up"] = din("w_up", [2, 2 * c_.NF, 128, KT, 128])
    I["w_down"] = din("w_down", [2, KT, 128, c_.NF, 128])
    I["g_mix"] = din("g_mix", [2, 128, KT])
    I["g_ffn"] = din("g_ffn", [2, 128, KT])
    I["g_kv"] = din("g_kv", [128, KT])
    I["g_mem"] = din("g_mem", [2, 128, D])
    I["g_mem_k"] = din("g_mem_k", [2, 128, c_.MHD])
    I["g_mem_q"] = din("g_mem_q", [2, 128, c_.MHD // 128])
    I["g_mem_kp"] = din("g_mem_kp", [2, 128, c_.MHD // 128])
    I["wconv"] = din("wconv", [128, c_.NXB, 4])
    I["bconv"] = din("bconv", [128, c_.NXB])
    I["dtb"] = din("dtb", [c_.HEADS, 1])
    I["alog"] = din("alog", [c_.HEADS, 1])
    I["dsk"] = din("dsk", [128, c_.NZ])
    I["gout"] = din("gout", [128, c_.NZ])
    I["gk"] = din("gk", [128, 3])
    I["gq"] = din("gq", [128, 3])
    I["wfc"] = din("wfc", [2, 128, c_.NF, 3])
    I["bfc"] = din("bfc", [2, 128, c_.NF])
    I["cosf"] = din("cosf", [NPASS, 128, NT])
    I["sinf"] = din("sinf", [NPASS, 128, NT])
    I["consts"] = din("consts", [5, 128, 128])
    O = {}
    O["y_p"] = dout("y_p", [SEQ, D])
    O["y_s"] = dout("y_s", [1, D])
    O["ssm_p"] = dout("ssm_p", [c_.INNER, 128])
    O["ssm_s"] = dout("ssm_s", [c_.INNER, 128])
    O["conv_p"] = dout("conv_p", [3, c_.XBC])
    O["conv_s"] = dout("conv_s", [3, c_.XBC])
    O["ffn_p"] = dout("ffn_p", [2, 2, c_.DFF])
    O["ffn_s"] = dout("ffn_s", [2, 2, c_.DFF])
    O["memkv_p"] = dout("memkv_p", [2, 256, 2 * c_.MEMW])
    for g in range(3):
        O["win%d_p" % g] = dout("win%d_p" % g, [c_.WL[g], 2048])
        O["win%d_s" % g] = dout("win%d_s" % g, [1, 2048])
    X = {}
    WC = {}
    for nm in ("w_in_a", "w_out_a", "w_kv", "w_in_b", "w_out_b"):
        WC[nm] = T(nc.dram_tensor("c_" + nm, list(I[nm].t.shape), BF16, kind="Internal").ap())
    WCL = {nm: [T(nc.dram_tensor("c_%s%d" % (nm, i), list(I[nm].t.shape[1:]), BF16, kind="Internal").ap()) for i in range(2)]
           for nm in ("w_up", "w_down")}
    X["xres"] = dscr("xres", [KT, 128, NT])
    X["kvs"] = dscr("kvs", [SEQ, 6144])
    X["kvs_s"] = dscr("kvs_s", [1, 6144])

    es = ExitStack()
    with es:
        es.enter_context(nc.allow_non_contiguous_dma(reason="single-token (N=1) sample pass moves 4-byte columns"))
        S = Sync(nc, es)

        uid = [0]

        def sb(name, shape, dt=F32, st=None):
            uid[0] += 1
            return T((st or es).enter_context(nc.sbuf_tensor("sb%d_%s" % (uid[0], name), list(shape), dt)))

        def ps(name, shape, dt=F32, st=None):
            uid[0] += 1
            return T((st or es).enter_context(nc.psum_tensor("ps%d_%s" % (uid[0], name), list(shape), dt)), psum=True)

        V = lambda f, r=(), w=(): S.op("dve", f, r, w)
        A = lambda f, r=(), w=(): S.op("act", f, r, w)
        PL = lambda f, r=(), w=(): S.op("pool", f, r, w)
        PE = lambda f, r=(), w=(), inc=True: S.op("pe", f, r, w, inc)

        cst = sb("cst", [128, 5, 128])
        cstb = sb("cstb", [128, 5, 128], BF16)
        S.dma("sp", cst[:], I["consts"].t.rearrange("k p m -> p k m"), w=[cst.b()])
        V(lambda: nc.vector.tensor_copy(out=cstb[:], in_=cst[:]), [cst.b()], [cstb.b()])
        ident, ones, Um, Lm, RTm = (cst[:, i, :] for i in range(5))
        identb, onesb = cstb[:, 0, :], cstb[:, 1, :]
        CB = [cst.b(), cstb.b()]

        def ld(name, src, shape):
            t = sb(name, shape)
            S.dma("sp", t[:], src, w=[t.b()])
            return t

        gmix = [ld("gmix%d" % i, I["g_mix"].t[i], [128, KT]) for i in range(2)]
        gffn = [ld("gffn%d" % i, I["g_ffn"].t[i], [128, KT]) for i in range(2)]
        gkv = ld("gkv", I["g_kv"].t, [128, KT])
        gmq = [ld("gmq%d" % i, I["g_mem_q"].t[i], [128, c_.MHD // 128]) for i in range(2)]
        gmkp = [ld("gmkp%d" % i, I["g_mem_kp"].t[i], [128, c_.MHD // 128]) for i in range(2)]
        wconv = ld("wconv", I["wconv"].t, [128, c_.NXB, 4])
        bconv = ld("bconv", I["bconv"].t, [128, c_.NXB])
        dtb = ld("dtb", I["dtb"].t, [c_.HEADS, 1])
        alog = ld("alog", I["alog"].t, [c_.HEADS, 1])
        dsk = ld("dsk", I["dsk"].t, [128, c_.NZ])
        gout = ld("gout", I["gout"].t, [128, c_.NZ])
        gk = ld("gk", I["gk"].t, [128, 3])
        gq = ld("gq", I["gq"].t, [128, 3])
        wfc = [ld("wfc%d" % i, I["wfc"].t[i], [128, c_.NF, 3]) for i in range(2)]
        bfc = [ld("bfc%d" % i, I["bfc"].t[i], [128, c_.NF]) for i in range(2)]
        aneg = sb("aneg", [c_.HEADS, 1])
        A(lambda: nc.scalar.activation(out=aneg[:], in_=alog[:], func=AF.Exp), [alog.b()], [aneg.b()])
        V(lambda: nc.vector.tensor_scalar(out=aneg[:], in0=aneg[:], scalar1=-1.0, scalar2=None, op0=ALU.mult),
          [aneg.b()], [aneg.b()])
        Sst = sb("Sst", [128, c_.INNER])
        Sbf = sb("Sbf", [128, c_.INNER], BF16)
        ctail = sb("ctail", [128, c_.NXB, 3])
        ftail = [sb("ftail%d" % i, [128, c_.NF, 2]) for i in range(2)]
        cosf = sb("cosf", [128, NT])
        sinf = sb("sinf", [128, NT])
        KC = 16
        NS, NB = 3, 2
        wst = [sb("wst%d" % i, [128, KC, 128]) for i in range(NS)]
        wbf = [sb("wbf%d" % i, [128, KC, 128], BF16) for i in range(NB)]
        wctr = [0]

        xres = X["xres"]

        wmode = ["off"]
        ring = [wbf[0], wbf[1]]
        for i_ in range(NS):
            v_ = wst[i_].t[:].rearrange("p k m -> p (k m)").bitcast(BF16)
            for h_ in range(2):
                ring.append(T(v_[:, h_ * KC * 128:(h_ + 1) * KC * 128].rearrange("p (k m) -> p k m", m=128)))
        rctr = [0]

        def gemm(wd, wdb, nblk, nkt, mov, movb, N, psums, evac, Ms=None, wc=None):
            mode = wmode[0] if wc is not None else "off"
            chunks = []
            for j in range(nblk):
                for k0 in range(0, nkt, KC):
                    chunks.append((j, k0, min(KC, nkt - k0)))

            def mms(i, u):
                j, k0, kc = chunks[i]
                pst = psums[j % len(psums)]
                M = 128 if Ms is None else Ms[j]
                for k in range(kc):
                    kt = k0 + k
                    last = (kt == nkt - 1)
                    PE(lambda: nc.tensor.matmul(pst[0:M, 0:N], lhsT=u[:, k, 0:M], rhs=mov(kt),
                                                start=(kt == 0), stop=last),
                       [u.b(), movb(kt)], [pst.b()], inc=(last or k == kc - 1))
                if k0 + kc == nkt:
                    evac(j, pst, M)

            if mode == "use":
                NR = len(ring)
                LDR = NR - 2
                ids = []
                for _ in chunks:
                    ids.append(rctr[0])
                    rctr[0] += 1

                def emit_ld(i):
                    j, k0, kc = chunks[i]
                    u = ring[ids[i] % NR]
                    S.dma("sp", u[:, 0:kc, :], wc.t[j, :, k0:k0 + kc, :], r=[wc.b()], w=[u.b()])
                for i in range(min(LDR, len(chunks))):
                    emit_ld(i)
                for i in range(len(chunks)):
                    if i + LDR < len(chunks):
                        emit_ld(i + LDR)
                    mms(i, ring[ids[i] % NR])
                return

            ids = []
            for (j, k0, kc) in chunks:
                ids.append(wctr[0])
                wctr[0] += 1

            def emit_dma(i):
                j, k0, kc = chunks[i]
                t = wst[ids[i] % NS]
                S.dma("sp", t[:, 0:kc, :], wd[j, :, k0:k0 + kc, :], r=[wdb], w=[t.b()])

            def emit_cast(i):
                j, k0, kc = chunks[i]
                t = wst[ids[i] % NS]
                u = wbf[ids[i] % NB]
                A(lambda: nc.scalar.activation(out=u[:, 0:kc, :], in_=t[:, 0:kc, :], func=AF.Copy), [t.b()], [u.b()])
                if mode == "fill":
                    S.dma("pool", wc.t[j, :, k0:k0 + kc, :], u[:, 0:kc, :], r=[u.b()], w=[wc.b()])

            LD, LC = 2, 1
            for i in range(min(LD, len(chunks))):
                emit_dma(i)
            for i in range(min(LC, len(chunks))):
                emit_cast(i)
            for i in range(len(chunks)):
                if i + LD < len(chunks):
                    emit_dma(i + LD)
                if i + LC < len(chunks):
                    emit_cast(i + LC)
                mms(i, wbf[ids[i] % NB])

        def norm_stream(gp, hT, N, pstat, xb, tmp, rstd):
            for kt in range(KT):
                x_ = xb[kt % 4]
                S.dma("pool", x_[:, 0:N], xres.t[kt, :, 0:N], r=[xres.b(kt)], w=[x_.b()])
                sq = tmp[kt % 2]
                A(lambda: nc.scalar.activation(out=sq[:, 0:N], in_=x_[:, 0:N], func=AF.Square), [x_.b()], [sq.b()])
                PE(lambda: nc.tensor.matmul(pstat[:, 0:N], lhsT=onesb, rhs=sq[:, 0:N], start=(kt == 0),
                                            stop=(kt == KT - 1)), [sq.b()] + CB, [pstat.b()], inc=True)
            A(lambda: nc.scalar.activation(out=rstd[:, 0:N], in_=pstat[:, 0:N], func=AF.Sqrt, scale=1.0 / D, bias=EPS),
              [pstat.b()], [rstd.b()])
            V(lambda: nc.vector.reciprocal(out=rstd[:, 0:N], in_=rstd[:, 0:N]), [rstd.b()], [rstd.b()])
            for kt in range(KT):
                x_ = xb[kt % 4]
                S.dma("pool", x_[:, 0:N], xres.t[kt, :, 0:N], r=[xres.b(kt)], w=[x_.b()])
                V(lambda: nc.vector.scalar_tensor_tensor(out=hT[:, kt, 0:N], in0=x_[:, 0:N],
                                                         scalar=gp[:, kt:kt + 1], in1=rstd[:, 0:N],
                                                         op0=ALU.mult, op1=ALU.mult),
                  [x_.b(), gp.b(), rstd.b()], [hT.b(kt)])

        def rstd_from(pst, out, N, n, parts=128):
            A(lambda: nc.scalar.activation(out=out[0:parts, 0:N], in_=pst[0:parts, 0:N], func=AF.Sqrt,
                                           scale=1.0 / n, bias=EPS), [pst.b()], [out.b()])
            V(lambda: nc.vector.reciprocal(out=out[0:parts, 0:N], in_=out[0:parts, 0:N]), [out.b()], [out.b()])

        def load_xres(XB, N):
            S.dma("pool", XB[:, :, 0:N], xres.t.rearrange("k p n -> p k n")[:, :, 0:N],
                  r=xres.bs(range(KT)), w=XB.bs(range(KT)))

        def make_resid_evac(N, xin, xo):
            ctr = [0]

            def evac(j, pst, M):
                i = ctr[0] % 2
                ctr[0] += 1
                S.dma("pool", xin[i][:, 0:N], xres.t[j, :, 0:N], r=[xres.b(j)], w=[xin[i].b()])
                V(lambda: nc.vector.tensor_tensor(out=xo[i][:, 0:N], in0=pst[:, 0:N], in1=xin[i][:, 0:N], op=ALU.add),
                  [pst.b(), xin[i].b()], [xo[i].b()])
                S.dma("pool", xres.t[j, :, 0:N], xo[i][:, 0:N], r=[xo[i].b()], w=[xres.b(j)])
            return evac

        def mem_kv_prompt():
            with ExitStack() as st:
                mtok = sb("mtok", [128, D], st=st)
                gm = sb("gm", [128, D], st=st)
                junk = sb("mjunk", [128, D], st=st)
                ss = sb("mss", [128, 1], st=st)
                memT = sb("memT", [128, KT, 256], BF16, st=st)
                kvf = sb("mkvf", [128, 2, 256], st=st)
                tmp = [sb("mtmp%d" % i, [128, 256], BF16, st=st) for i in range(2)]
                rs = sb("mrs", [128, 256], st=st)
                stg = sb("mstg", [128, 2, 128], st=st)
                gmk = sb("gmk", [128, c_.MHD], st=st)
                ptr = ps("mptr", [128, 512], st=st)
                pg = [ps("mpg%d" % i, [128, 512], st=st) for i in range(2)]
                pn = ps("mpn", [128, 512], st=st)
                pt2 = ps("mpt2", [128, 512], st=st)
                nb = c_.MHD // 128
                for li in range(2):
                    S.dma("sp", gm[:], I["g_mem"].t[li], w=[gm.b()])
                    S.dma("sp", gmk[:], I["g_mem_k"].t[li], w=[gmk.b()])
                    for mt in range(2):
                        S.dma("sp", mtok[:], I["memp"].t[mt * 128:(mt + 1) * 128, :], w=[mtok.b()])
                        import os as _os
                        lvl = int(_os.environ.get("KLVL", "9"))
                        if lvl < 1:
                            continue
                        A(lambda: nc.scalar.activation(out=junk[:], in_=mtok[:], func=AF.Square), [mtok.b()], [junk.b()])
                        if lvl < 2:
                            continue
                        V(lambda: nc.vector.tensor_reduce(out=ss[:], in_=junk[:], axis=mybir.AxisListType.X, op=ALU.add),
                          [junk.b()], [ss.b()])
                        if lvl < 3:
                            continue
                        A(lambda: nc.scalar.activation(out=ss[:], in_=ss[:], func=AF.Sqrt, scale=1.0 / D, bias=EPS),
                          [ss.b()], [ss.b()])
                        V(lambda: nc.vector.reciprocal(out=ss[:], in_=ss[:]), [ss.b()], [ss.b()])
                        if lvl < 4:
                            continue
                        V(lambda: nc.vector.scalar_tensor_tensor(out=mtok[:], in0=mtok[:], scalar=ss[:, 0:1], in1=gm[:],
                                                                 op0=ALU.mult, op1=ALU.mult),
                          [mtok.b(), ss.b(), gm.b()], [mtok.b()])
                        if lvl < 5:
                            continue
                        for k4 in range(0, KT, 4):
                            for q in range(4):
                                kt = k4 + q
                                PE(lambda: nc.tensor.transpose(ptr[:, q * 128:(q + 1) * 128], mtok[:, kt * 128:(kt + 1) * 128], ident),
                                   [mtok.b()] + CB, [ptr.b()], inc=(q == 3))
                            V(lambda: nc.vector.tensor_copy(out=memT[:, k4:k4 + 4, mt * 128:(mt + 1) * 128],
                                                            in_=ptr[:].rearrange("p (q m) -> p q m", q=4)),
                              [ptr.b()], memT.bs(range(k4, k4 + 4)))
                    nblk = 2 * c_.NQM
                    outd = O["memkv_p"]

                    def put_tok(src, col0):
                        for mt in range(2):
                            PE(lambda: nc.tensor.transpose(pt2[:, mt * 128:(mt + 1) * 128], src[:, mt * 128:(mt + 1) * 128], ident),
                               [kvf.b()] + CB, [pt2.b()], inc=(mt == 1))
                        V(lambda: nc.vector.tensor_copy(out=stg[:], in_=pt2[:, 0:256].rearrange("p (a m) -> p a m", a=2)),
                          [pt2.b()], [stg.b()])
                        S.dma("pool", outd.t[li, :, col0:col0 + 128].rearrange("(a p) m -> p a m", a=2), stg[:],
                              r=[stg.b()], w=[outd.b(li)])

                    def evac(j, pst, M):
                        if j < c_.NQM:
                            bi = j % nb
                            V(lambda: nc.vector.tensor_copy(out=kvf[:, bi, :], in_=pst[:, 0:256]), [pst.b()], [kvf.b()])
                            sq = tmp[bi % 2]
                            A(lambda: nc.scalar.activation(out=sq[:], in_=pst[:, 0:256], func=AF.Square), [pst.b()], [sq.b()])
                            if bi == nb - 1:
                                for b2 in range(nb):
                                    PE(lambda: nc.tensor.matmul(pn[:, 0:256], lhsT=onesb, rhs=tmp[b2 % 2][:], start=(b2 == 0), stop=(b2 == nb - 1)),
                                       [tmp[b2 % 2].b()] + CB, [pn.b()], inc=(b2 == nb - 1))
                                rstd_from(pn, rs, 256, c_.MHD)
                                for b2 in range(nb):
                                    V(lambda: nc.vector.scalar_tensor_tensor(out=kvf[:, b2, :], in0=kvf[:, b2, :], scalar=gmkp[li][:, b2:b2 + 1],
                                                                             in1=rs[:, 0:256], op0=ALU.mult, op1=ALU.mult),
                                      [kvf.b(), rs.b(), gmkp[li].b()], [kvf.b()])
                                    jj = j - (nb - 1) + b2
                                    put_tok(kvf[:, b2, :], jj * 128)
                        else:
                            V(lambda: nc.vector.tensor_copy(out=kvf[:, 0, :], in_=pst[:, 0:256]), [pst.b()], [kvf.b()])
                            put_tok(kvf[:, 0, :], j * 128)

                    import os as _os
                    ksub = int(_os.environ.get("KSUB", "9"))

                    def evac_dbg(j, pst, M):
                        V(lambda: nc.vector.tensor_copy(out=kvf[:, 0, :], in_=pst[:, 0:256]), [pst.b()], [kvf.b()])
                    if ksub >= 2:
                        gemm(I["w_mem_kv"].t[li], I["w_mem_kv"].b(), nblk, KT, lambda kt: memT[:, kt, :], lambda kt: memT.b(kt),
                             256, pg, evac if ksub >= 3 else evac_dbg)
                S.barrier()

        def trunk(pi, N, c, x_src, y_dst, kv_dst, kv_base, memkv_src, final, sample):
            nch = N // c
            t0 = kv_base
            with ExitStack() as st:
                XB = sb("XB0", [128, KT, NT], st=st)
                xtok = sb("xtok", [128, D], st=st)
                ptr = [ps("p0tr%d" % i, [128, 512], st=st) for i in range(2)]
                for ci in range(nch):
                    S.dma("sp", xtok[0:c, :], x_src.t[ci * c:(ci + 1) * c, :], w=[xtok.b()])
                    for k4 in range(0, KT, 4):
                        p = ptr[(k4 // 4) % 2]
                        for q in range(4):
                            kt = k4 + q
                            PE(lambda: nc.tensor.transpose(p[:, q * 128:q * 128 + c], xtok[0:c, kt * 128:(kt + 1) * 128], ident[0:c, 0:c]),
                               [xtok.b()] + CB, [p.b()], inc=(q == 3))
                        V(lambda: nc.vector.tensor_copy(out=XB[:, k4:k4 + 4, ci * c:(ci + 1) * c],
                                                        in_=p[:].rearrange("p (q m) -> p q m", q=4)[:, :, 0:c]),
                          [p.b()], XB.bs(range(k4, k4 + 4)))
                S.dma("pool", xres.t.rearrange("k p n -> p k n")[:, :, 0:N], XB[:, :, 0:N],
                      r=XB.bs(range(KT)), w=xres.bs(range(KT)))
                S.barrier()

            S.dma("sp", cosf[:], I["cosf"].t[pi], w=[cosf.b()])
            S.dma("sp", sinf[:], I["sinf"].t[pi], w=[sinf.b()])

            def norm_phase(gp, hT, st0):
                with ExitStack() as st:
                    xb = [sb("nxb%d" % i, [128, NT], st=st) for i in range(4)]
                    tmp = [sb("ntmp%d" % i, [128, NT], BF16, st=st) for i in range(2)]
                    rstd = sb("nrstd", [128, NT], st=st)
                    pstat = ps("npstat", [128, 512], st=st)
                    norm_stream(gp, hT, N, pstat, xb, tmp, rstd)
                    S.barrier()

            def normrope(pst, gcol, outap, outb, wk, pn, pr):
                raw, sq, rs, t1 = wk
                V(lambda: nc.vector.tensor_copy(out=raw[:, 0:N], in_=pst[:, 0:N]), [pst.b()], [raw.b()])
                A(lambda: nc.scalar.activation(out=sq[:, 0:N], in_=pst[:, 0:N], func=AF.Square), [pst.b()], [sq.b()])
                PE(lambda: nc.tensor.matmul(pn[:, 0:N], lhsT=ones, rhs=sq[:, 0:N], start=True, stop=True),
                   [sq.b()] + CB, [pn.b()])
                rstd_from(pn, rs, N, 128)
                V(lambda: nc.vector.scalar_tensor_tensor(out=raw[:, 0:N], in0=raw[:, 0:N], scalar=gcol, in1=rs[:, 0:N],
                                                         op0=ALU.mult, op1=ALU.mult),
                  [raw.b(), rs.b(), gk.b(), gq.b()], [raw.b()])
                PE(lambda: nc.tensor.matmul(pr[:, 0:N], lhsT=RTm, rhs=raw[:, 0:N], start=True, stop=True),
                   [raw.b()] + CB, [pr.b()])
                V(lambda: nc.vector.tensor_tensor(out=t1[:, 0:N], in0=pr[:, 0:N], in1=sinf[:, 0:N], op=ALU.mult),
                  [pr.b(), sinf.b()], [t1.b()])
                V(lambda: nc.vector.tensor_tensor(out=raw[:, 0:N], in0=raw[:, 0:N], in1=cosf[:, 0:N], op=ALU.mult),
                  [raw.b(), cosf.b()], [raw.b()])
                V(lambda: nc.vector.tensor_tensor(out=outap, in0=raw[:, 0:N], in1=t1[:, 0:N], op=ALU.add),
                  [raw.b(), t1.b()], [outb])

            def mem_attn(li, qm, ymem, st):
                nb = c_.MHD // 128
                kvm = sb("kvm", [128, 2, c_.MEMW], st=st)
                KTb = sb("KTb", [128, nb, 4, 256], BF16, st=st)
                Vb = sb("Vb", [128, 2, c_.MEMW], BF16, st=st)
                qn = sb("qn", [128, c_.NQM, NT], BF16, st=st)
                sq = [sb("masq%d" % i, [128, NT], BF16, st=st) for i in range(2)]
                rs = sb("mars", [128, NT], st=st)
                PT = sb("maPT", [128, 2, NT], BF16, st=st)
                rden = sb("marden", [128, NT], st=st)
                ptk = ps("maptk", [128, 512], st=st)
                pn = ps("mapn", [128, 512], st=st)
                pS = [ps("mapS%d" % i, [128, 512], st=st) for i in range(2)]
                pd = ps("mapd", [128, 512], st=st)
                pO = [ps("mapO%d" % i, [128, 512], st=st) for i in range(2)]
                S.dma("sp", kvm[:], memkv_src.t[li].rearrange("(a p) m -> p a m", a=2)[:, :, 0:c_.MEMW], r=[memkv_src.b(li)], w=[kvm.b()])
                for hd in range(4):
                    for bl in range(nb):
                        for mt in range(2):
                            col = hd * c_.MHD + bl * 128
                            PE(lambda: nc.tensor.transpose(ptk[:, mt * 128:(mt + 1) * 128], kvm[:, mt, col:col + 128], ident),
                               [kvm.b()] + CB, [ptk.b()], inc=(mt == 1))
                        V(lambda: nc.vector.tensor_copy(out=KTb[:, bl, hd, :], in_=ptk[:, 0:256]), [ptk.b()], [KTb.b()])
                S.dma("sp", kvm[:], memkv_src.t[li].rearrange("(a p) m -> p a m", a=2)[:, :, c_.MEMW:2 * c_.MEMW], r=[memkv_src.b(li)], w=[kvm.b()])
                A(lambda: nc.scalar.activation(out=Vb[:], in_=kvm[:], func=AF.Copy), [kvm.b()], [Vb.b()])
                for hd in range(4):
                    for bl in range(nb):
                        j = hd * nb + bl
                        s_ = sq[bl % 2]
                        A(lambda: nc.scalar.activation(out=s_[:, 0:N], in_=qm[:, j, 0:N], func=AF.Square), [qm.b(j)], [s_.b()])
                        PE(lambda: nc.tensor.matmul(pn[:, 0:N], lhsT=onesb, rhs=s_[:, 0:N], start=(bl == 0), stop=(bl == nb - 1)),
                           [s_.b()] + CB, [pn.b()])
                    rstd_from(pn, rs, N, c_.MHD)
                    for bl in range(nb):
                        j = hd * nb + bl
                        V(lambda: nc.vector.scalar_tensor_tensor(out=qn[:, j, 0:N], in0=qm[:, j, 0:N], scalar=gmq[li][:, bl:bl + 1],
                                                                 in1=rs[:, 0:N], op0=ALU.mult, op1=ALU.mult),
                          [qm.b(j), rs.b(), gmq[li].b()], [qn.b(j)])
                scale = c_.MHD ** -0.5
                for hd in range(4):
                    for mt in range(2):
                        p = pS[mt]
                        for bl in range(nb):
                            PE(lambda: nc.tensor.matmul(p[:, 0:N], lhsT=KTb[:, bl, hd, mt * 128:(mt + 1) * 128], rhs=qn[:, hd * nb + bl, 0:N],
                                                        start=(bl == 0), stop=(bl == nb - 1)),
                               [KTb.b(), qn.b(hd * nb + bl)], [p.b()], inc=(bl == nb - 1))
                        A(lambda: nc.scalar.activation(out=PT[:, mt, 0:N], in_=p[:, 0:N], func=AF.Exp, scale=scale), [p.b()], [PT.b(mt)])
                    for mt in range(2):
                        PE(lambda: nc.tensor.matmul(pd[:, 0:N], lhsT=onesb, rhs=PT[:, mt, 0:N], start=(mt == 0), stop=(mt == 1)),
                           [PT.b(mt)] + CB, [pd.b()], inc=(mt == 1))
                    V(lambda: nc.vector.reciprocal(out=rden[:, 0:N], in_=pd[:, 0:N]), [pd.b()], [rden.b()])
                    for bl in range(nb):
                        p = pO[bl % 2]
                        for mt in range(2):
                            col = hd * c_.MHD + bl * 128
                            PE(lambda: nc.tensor.matmul(p[:, 0:N], lhsT=Vb[:, mt, col:col + 128], rhs=PT[:, mt, 0:N],
                                                        start=(mt == 0), stop=(mt == 1)),
                               [Vb.b(), PT.b(mt)], [p.b()], inc=(mt == 1))
                        j = hd * nb + bl
                        V(lambda: nc.vector.tensor_tensor(out=ymem[:, j, 0:N], in0=p[:, 0:N], in1=rden[:, 0:N], op=ALU.mult),
                          [p.b(), rden.b()], [ymem.b(j)])

            def ffn(li):
                with ExitStack() as st:
                    hT = sb("fhT", [128, KT, NT], BF16, st=st)
                    norm_phase(gffn[li], hT, st)
                    act = sb("fact", [128, c_.NF, NT], BF16, st=st)
                    pad = [sb("fpad%d" % i, [128, NT + 2], st=st) for i in range(2)]
                    acc = [sb("facc%d" % i, [128, NT], st=st) for i in range(2)]
                    gs = [sb("fgs%d" % i, [128, NT], st=st) for i in range(2)]
                    pg = [ps("fpg%d" % i, [128, 512], st=st) for i in range(4)]
                    ft = ftail[li]

                    def evac_up(j, pst, M):
                        jf = j // 2
                        i = jf % 2
                        if j % 2 == 0:
                            pd_, ac = pad[i], acc[i]
                            V(lambda: nc.vector.tensor_copy(out=pd_[:, 0:2], in_=ft[:, jf, :]), [ft.b(jf)], [pd_.b()])
                            V(lambda: nc.vector.tensor_copy(out=pd_[:, 2:2 + N], in_=pst[:, 0:N]), [pst.b()], [pd_.b()])
                            V(lambda: nc.vector.tensor_copy(out=ft[:, jf, :], in_=pd_[:, N:N + 2]), [pd_.b()], [ft.b(jf)])
                            V(lambda: nc.vector.tensor_scalar(out=ac[:, 0:N], in0=pd_[:, 0:N], scalar1=wfc[li][:, jf, 0:1],
                                                              scalar2=bfc[li][:, jf:jf + 1], op0=ALU.mult, op1=ALU.add),
                              [pd_.b(), wfc[li].b(), bfc[li].b()], [ac.b()])
                            for k in (1, 2):
                                V(lambda: nc.vector.scalar_tensor_tensor(out=ac[:, 0:N], in0=pd_[:, k:k + N], scalar=wfc[li][:, jf, k:k + 1],
                                                                         in1=ac[:, 0:N], op0=ALU.mult, op1=ALU.add),
                                  [pd_.b(), ac.b()], [ac.b()])
                            A(lambda: nc.scalar.activation(out=gs[i][:, 0:N], in_=ac[:, 0:N], func=AF.Silu), [ac.b()], [gs[i].b()])
                        else:
                            V(lambda: nc.vector.tensor_tensor(out=act[:, jf, 0:N], in0=pst[:, 0:N], in1=gs[i][:, 0:N], op=ALU.mult),
                              [pst.b(), gs[i].b()], [act.b(jf)])

                    gemm(I["w_up"].t[li], I["w_up"].b(), 2 * c_.NF, KT, lambda kt: hT[:, kt, 0:N], lambda kt: hT.b(kt), N, pg, evac_up, wc=WCL["w_up"][li])
                    xin = [sb("fxin%d" % i, [128, NT], st=st) for i in range(2)]
                    xo = [sb("fxo%d" % i, [128, NT], st=st) for i in range(2)]
                    gemm(I["w_down"].t[li], I["w_down"].b(), KT, c_.NF, lambda kt: act[:, kt, 0:N], lambda kt: act.b(kt), N, pg[0:2],
                         make_resid_evac(N, xin, xo), wc=WCL["w_down"][li])
                    S.barrier()

            with ExitStack() as st:
                sz = sb("asz", [128, c_.NZ, NT], BF16, st=st)
                xbc = sb("axbc", [128, c_.NXB, NT], BF16, st=st)
                qm = sb("aqm", [128, c_.NQM, NT], st=st)
                ymem = sb("aymem", [128, c_.NQM, NT], BF16, st=st)
                dtT = sb("adtT", [c_.HEADS, NT], st=st)
                laT = sb("alaT", [c_.HEADS, NT], st=st)
                with ExitStack() as st2:
                    hT = sb("ahT", [128, KT, NT], BF16, st=st2)
                    norm_phase(gmix[0], hT, st2)
                    pad = [sb("apad%d" % i, [128, NT + 3], st=st2) for i in range(2)]
                    acc = [sb("aacc%d" % i, [128, NT], st=st2) for i in range(2)]
                    pg = [ps("apg%d" % i, [128, 512], st=st2) for i in range(3)]
                    cc = [0]

                    def evacA(j, pst, M):
                        if j < c_.NZ:
                            A(lambda: nc.scalar.activation(out=sz[:, j, 0:N], in_=pst[:, 0:N], func=AF.Silu), [pst.b()], [sz.b(j)])
                        elif j < c_.NZ + c_.NXB:
                            jb = j - c_.NZ
                            i = cc[0] % 2
                            cc[0] += 1
                            pd_, ac = pad[i], acc[i]
                            V(lambda: nc.vector.tensor_copy(out=pd_[:, 0:3], in_=ctail[:, jb, :]), [ctail.b(jb)], [pd_.b()])
                            V(lambda: nc.vector.tensor_copy(out=pd_[:, 3:3 + N], in_=pst[:, 0:N]), [pst.b()], [pd_.b()])
                            V(lambda: nc.vector.tensor_copy(out=ctail[:, jb, :], in_=pd_[:, N:N + 3]), [pd_.b()], [ctail.b(jb)])
                            V(lambda: nc.vector.tensor_scalar(out=ac[:, 0:N], in0=pd_[:, 0:N], scalar1=wconv[:, jb, 0:1],
                                                              scalar2=bconv[:, jb:jb + 1], op0=ALU.mult, op1=ALU.add),
                              [pd_.b(), wconv.b(), bconv.b()], [ac.b()])
                            for k in (1, 2, 3):
                                V(lambda: nc.vector.scalar_tensor_tensor(out=ac[:, 0:N], in0=pd_[:, k:k + N], scalar=wconv[:, jb, k:k + 1],
                                                                         in1=ac[:, 0:N], op0=ALU.mult, op1=ALU.add),
                                  [pd_.b(), ac.b()], [ac.b()])
                            A(lambda: nc.scalar.activation(out=xbc[:, jb, 0:N], in_=ac[:, 0:N], func=AF.Silu), [ac.b()], [xbc.b(jb)])
                        elif j == c_.NZ + c_.NXB:
                            H = c_.HEADS
                            A(lambda: nc.scalar.activation(out=dtT[:, 0:N], in_=pst[0:H, 0:N], func=AF.Exp, bias=dtb[:, 0:1]),
                              [pst.b(), dtb.b()], [dtT.b()])
                            A(lambda: nc.scalar.activation(out=dtT[:, 0:N], in_=dtT[:, 0:N], func=AF.Ln, bias=1.0), [dtT.b()], [dtT.b()])
                            V(lambda: nc.vector.tensor_scalar(out=laT[:, 0:N], in0=dtT[:, 0:N], scalar1=aneg[:, 0:1], scalar2=None, op0=ALU.mult),
                              [dtT.b(), aneg.b()], [laT.b()])
                        else:
                            jq = j - (c_.NZ + c_.NXB + 1)
                            V(lambda: nc.vector.tensor_copy(out=qm[:, jq, 0:N], in_=pst[:, 0:N]), [pst.b()], [qm.b(jq)])

                    Ms = [128] * (c_.NZ + c_.NXB) + [c_.HEADS] + [128] * c_.NQM
                    gemm(I["w_in_a"].t, I["w_in_a"].b(), NBA, KT, lambda kt: hT[:, kt, 0:N], lambda kt: hT.b(kt), N, pg, evacA, Ms, wc=WC["w_in_a"])
                    S.barrier()
                with ExitStack() as st2:
                    H = c_.HEADS
                    dtk = sb("sdtk", [128, H], st=st2)
                    lak = sb("slak", [128, H], st=st2)
                    cum = sb("scum", [128, H], st=st2)
                    wend = sb("swend", [128, H], st=st2)
                    ecl = sb("secl", [128, H], st=st2)
                    xdt = sb("sxdt", [128, c_.INNER], BF16, st=st2)
                    xdtw = sb("sxdtw", [128, c_.INNER], BF16, st=st2)
                    Btok = sb("sBtok", [128, c_.G, 128], BF16, st=st2)
                    cbm = sb("scbm", [128, 128], st=st2)
                    seg = [sb("sseg%d" % i, [128, 128], st=st2) for i in range(2)]
                    LT = [sb("sLT%d" % i, [128, 128], BF16, st=st2) for i in range(2)]
                    ecum = [sb("secum%d" % i, [128, 128], st=st2) for i in range(2)]
                    Cs = [sb("sCs%d" % i, [128, 128], BF16, st=st2) for i in range(2)]
                    ytmp = sb("sytmp", [128, 128], st=st2)
                    stmp = sb("sstmp", [128, 384], st=st2)
                    ptr = ps("sptr", [128, 1024], BF16, st=st2)
                    psm = ps("spsm", [128, 512], st=st2)
                    pcb = ps("spcb", [128, 512], st=st2)
                    pcum = [ps("spcum%d" % i, [128, 512], st=st2) for i in range(2)]
                    py = [ps("spy%d" % i, [128, 512], st=st2) for i in range(2)]
                    pds = ps("spds", [128, 512], st=st2)
                    for ci in range(nch):
                        cs = slice(ci * c, (ci + 1) * c)
                        PE(lambda: nc.tensor.transpose(psm[0:c, 0:H], dtT[:, cs], ident[0:H, 0:H]), [dtT.b()] + CB, [psm.b()], inc=False)
                        PE(lambda: nc.tensor.transpose(psm[0:c, 64:64 + H], laT[:, cs], ident[0:H, 0:H]), [laT.b()] + CB, [psm.b()])
                        V(lambda: nc.vector.tensor_copy(out=dtk[0:c, :], in_=psm[0:c, 0:H]), [psm.b()], [dtk.b()])
                        V(lambda: nc.vector.tensor_copy(out=lak[0:c, :], in_=psm[0:c, 64:64 + H]), [psm.b()], [lak.b()])
                        PE(lambda: nc.tensor.matmul(psm[0:c, 128:128 + H], lhsT=Um[0:c, 0:c], rhs=lak[0:c, :], start=True, stop=True),
                           [lak.b()] + CB, [psm.b()], inc=False)
                        PE(lambda: nc.tensor.matmul(psm[:, 192:192 + H], lhsT=ones[0:c, :], rhs=lak[0:c, :], start=True, stop=True),
                           [lak.b()] + CB, [psm.b()])
                        V(lambda: nc.vector.tensor_copy(out=cum[0:c, :], in_=psm[0:c, 128:128 + H]), [psm.b()], [cum.b()])
                        V(lambda: nc.vector.tensor_tensor(out=wend[0:c, :], in0=psm[0:c, 192:192 + H], in1=cum[0:c, :], op=ALU.subtract),
                          [psm.b(), cum.b()], [wend.b()])
                        A(lambda: nc.scalar.activation(out=wend[0:c, :], in_=wend[0:c, :], func=AF.Exp), [wend.b()], [wend.b()])
                        A(lambda: nc.scalar.activation(out=ecl[:], in_=psm[:, 192:192 + H], func=AF.Exp), [psm.b()], [ecl.b()])
                        for j4 in range(0, c_.NZ, 4):
                            nb4 = min(4, c_.NZ - j4)
                            for q in range(nb4):
                                PE(lambda: nc.tensor.transpose(ptr[0:c, q * 128:(q + 1) * 128], xbc[:, j4 + q, cs], identb),
                                   [xbc.b(j4 + q)] + CB, [ptr.b()], inc=(q == nb4 - 1))
                            hs = slice(2 * j4, 2 * j4 + 2 * nb4)
                            fs = slice(j4 * 128, (j4 + nb4) * 128)
                            V(lambda: nc.vector.tensor_tensor(
                                out=xdt[0:c, fs].rearrange("p (h d) -> p h d", d=64),
                                in0=ptr[0:c, 0:nb4 * 128].rearrange("p (h d) -> p h d", d=64),
                                in1=dtk[0:c, hs].unsqueeze(2).to_broadcast([c, 2 * nb4, 64]), op=ALU.mult),
                              [ptr.b(), dtk.b()], [xdt.b()])
                            V(lambda: nc.vector.tensor_tensor(
                                out=xdtw[0:c, fs].rearrange("p (h d) -> p h d", d=64),
                                in0=xdt[0:c, fs].rearrange("p (h d) -> p h d", d=64),
                                in1=wend[0:c, hs].unsqueeze(2).to_broadcast([c, 2 * nb4, 64]), op=ALU.mult),
                              [xdt.b(), wend.b()], [xdtw.b()])
                        for g4 in range(0, c_.G, 4):
                            nb4 = min(4, c_.G - g4)
                            for q in range(nb4):
                                PE(lambda: nc.tensor.transpose(ptr[0:c, q * 128:(q + 1) * 128], xbc[:, c_.NZ + g4 + q, cs], identb),
                                   [xbc.b(c_.NZ + g4 + q)] + CB, [ptr.b()], inc=(q == nb4 - 1))
                            V(lambda: nc.vector.tensor_copy(out=Btok[0:c, g4:g4 + nb4, :],
                                                            in_=ptr[0:c, 0:nb4 * 128].rearrange("p (g d) -> p g d", d=128)),
                              [ptr.b()], [Btok.b()])
                        for g in range(c_.G):
                            BTg = xbc[:, c_.NZ + g, cs]
                            CTg = xbc[:, c_.NZ + c_.G + g, cs]
                            PE(lambda: nc.tensor.matmul(pcb[0:c, 0:c], lhsT=BTg, rhs=CTg, start=True, stop=True),
                               [xbc.b(c_.NZ + g), xbc.b(c_.NZ + c_.G + g)], [pcb.b()])
                            V(lambda: nc.vector.tensor_tensor(out=cbm[0:c, 0:c], in0=pcb[0:c, 0:c], in1=Um[0:c, 0:c], op=ALU.mult),
                              [pcb.b()] + CB, [cbm.b()])
                            for hh in range(6):
                                h = g * 6 + hh
                                i = h % 2
                                pc = pcum[i]
                                PE(lambda: nc.tensor.matmul(pc[:, 0:c], lhsT=lak[0:c, h:h + 1].to_broadcast([c, 128]), rhs=Um[0:c, 0:c],
                                                            start=True, stop=True), [lak.b()] + CB, [pc.b()])
                                V(lambda: nc.vector.tensor_scalar(out=seg[i][0:c, 0:c], in0=pc[0:c, 0:c], scalar1=cum[0:c, h:h + 1], scalar2=0.0,
                                                                  op0=ALU.subtract, op1=ALU.min), [pc.b(), cum.b()], [seg[i].b()])
                                A(lambda: nc.scalar.activation(out=seg[i][0:c, 0:c], in_=seg[i][0:c, 0:c], func=AF.Exp), [seg[i].b()], [seg[i].b()])
                                V(lambda: nc.vector.tensor_tensor(out=LT[i][0:c, 0:c], in0=seg[i][0:c, 0:c], in1=cbm[0:c, 0:c], op=ALU.mult),
                                  [seg[i].b(), cbm.b()], [LT[i].b()])
                                A(lambda: nc.scalar.activation(out=ecum[i][:, 0:c], in_=pc[:, 0:c], func=AF.Exp), [pc.b()], [ecum[i].b()])
                                V(lambda: nc.vector.tensor_tensor(out=Cs[i][:, 0:c], in0=CTg, in1=ecum[i][:, 0:c], op=ALU.mult),
                                  [xbc.b(c_.NZ + c_.G + g), ecum[i].b()], [Cs[i].b()])
                                jb = h // 2
                                pyy = py[jb % 2]
                                po = (h % 2) * 64
                                PE(lambda: nc.tensor.matmul(pyy[po:po + 64, 0:c], lhsT=xdt[0:c, h * 64:(h + 1) * 64], rhs=LT[i][0:c, 0:c],
                                                            start=True, stop=False), [xdt.b(), LT[i].b()], [pyy.b()], inc=False)
                                PE(lambda: nc.tensor.matmul(pyy[po:po + 64, 0:c], lhsT=Sbf[:, h * 64:(h + 1) * 64], rhs=Cs[i][:, 0:c],
                                                            start=False, stop=True), [Sbf.b(), Cs[i].b()], [pyy.b()])
                                if h % 2 == 1:
                                    V(lambda: nc.vector.scalar_tensor_tensor(out=ytmp[:, 0:c], in0=xbc[:, jb, cs], scalar=dsk[:, jb:jb + 1],
                                                                             in1=pyy[:, 0:c], op0=ALU.mult, op1=ALU.add),
                                      [xbc.b(jb), dsk.b(), pyy.b()], [ytmp.b()])
                                    V(lambda: nc.vector.tensor_tensor(out=sz[:, jb, cs], in0=ytmp[:, 0:c], in1=sz[:, jb, cs], op=ALU.mult),
                                      [ytmp.b(), sz.b(jb)], [sz.b(jb)])
                        for g in range(c_.G):
                            gs_ = slice(g * 384, (g + 1) * 384)
                            PE(lambda: nc.tensor.matmul(pds[:, 0:384], lhsT=Btok[0:c, g, :], rhs=xdtw[0:c, gs_], start=True, stop=True),
                               [Btok.b(), xdtw.b()], [pds.b()])
                            V(lambda: nc.vector.tensor_tensor(
                                out=stmp[:].rearrange("p (h d) -> p h d", d=64), in0=Sst[:, gs_].rearrange("p (h d) -> p h d", d=64),
                                in1=ecl[:, g * 6:(g + 1) * 6].unsqueeze(2).to_broadcast([128, 6, 64]), op=ALU.mult),
                              [Sst.b(), ecl.b()], [stmp.b()])
                            V(lambda: nc.vector.tensor_tensor(out=Sst[:, gs_], in0=stmp[:], in1=pds[:, 0:384], op=ALU.add),
                              [stmp.b(), pds.b(), Sbf.b()], [Sst.b()])
                        V(lambda: nc.vector.tensor_copy(out=Sbf[:], in_=Sst[:]), [Sst.b()], [Sbf.b()])
                    S.barrier()
                with ExitStack() as st2:
                    sq = [sb("gsq%d" % i, [128, NT], BF16, st=st2) for i in range(2)]
                    rs = sb("grs", [128, NT], st=st2)
                    pn = ps("gpn", [128, 512], st=st2)
                    for g in range(c_.G):
                        for b3 in range(3):
                            j = g * 3 + b3
                            s_ = sq[b3 % 2]
                            A(lambda: nc.scalar.activation(out=s_[:, 0:N], in_=sz[:, j, 0:N], func=AF.Square), [sz.b(j)], [s_.b()])
                            PE(lambda: nc.tensor.matmul(pn[:, 0:N], lhsT=onesb, rhs=s_[:, 0:N], start=(b3 == 0), stop=(b3 == 2)),
                               [s_.b()] + CB, [pn.b()])
                        rstd_from(pn, rs, N, 384)
                        for b3 in range(3):
                            j = g * 3 + b3
                            V(lambda: nc.vector.scalar_tensor_tensor(out=sz[:, j, 0:N], in0=sz[:, j, 0:N], scalar=gout[:, j:j + 1], in1=rs[:, 0:N],
                                                                     op0=ALU.mult, op1=ALU.mult), [sz.b(j), rs.b(), gout.b()], [sz.b(j)])
                    S.barrier()
                with ExitStack() as st2:
                    mem_attn(0, qm, ymem, st2)
                    S.barrier()
                with ExitStack() as st2:
                    xin = [sb("axin%d" % i, [128, NT], st=st2) for i in range(2)]
                    xo = [sb("axo%d" % i, [128, NT], st=st2) for i in range(2)]
                    pg = [ps("aopg%d" % i, [128, 512], st=st2) for i in range(2)]
                    mv = lambda kt: (sz[:, kt, 0:N] if kt < c_.NZ else ymem[:, kt - c_.NZ, 0:N])
                    mvb = lambda kt: (sz.b(kt) if kt < c_.NZ else ymem.b(kt - c_.NZ))
                    gemm(I["w_out_a"].t, I["w_out_a"].b(), KT, c_.A_MIX // 128, mv, mvb, N, pg, make_resid_evac(N, xin, xo), wc=WC["w_out_a"])
                    S.barrier()
            ffn(0)
            with ExitStack() as st:
                hT = sb("khT", [128, KT, NT], BF16, st=st)
                norm_phase(gkv, hT, st)
                wk = [sb("kwk%d" % i, [128, NT], st=st) for i in range(4)]
                vraw = sb("kvraw", [128, NT], st=st)
                stg = [sb("kstg%d" % i, [128, NT // 128 if c == 128 else 1, 128], st=st) for i in range(2)]
                pg = [ps("kpg%d" % i, [128, 512], st=st) for i in range(3)]
                pn = ps("kpn", [128, 512], st=st)
                pr = ps("kpr", [128, 512], st=st)
                pt = [ps("kpt%d" % i, [128, 512], st=st) for i in range(2)]
                kc = [0]

                def evacK(j, pst, M):
                    g, kv = j // 16, (j // 8) % 2
                    i = kc[0] % 2
                    kc[0] += 1
                    if kv == 0:
                        normrope(pst, gk[:, g:g + 1], vraw[:, 0:N], vraw.b(), wk, pn, pr)
                    else:
                        V(lambda: nc.vector.tensor_copy(out=vraw[:, 0:N], in_=pst[:, 0:N]), [pst.b()], [vraw.b()])
                    p = pt[i]
                    for ci in range(nch):
                        PE(lambda: nc.tensor.transpose(p[0:c, ci * 128:(ci + 1) * 128], vraw[:, ci * c:(ci + 1) * c], ident),
                           [vraw.b()] + CB, [p.b()], inc=(ci == nch - 1))
                    V(lambda: nc.vector.tensor_copy(out=stg[i][0:c, :, :], in_=p[0:c, 0:nch * 128].rearrange("p (a m) -> p a m", m=128)),
                      [p.b()], [stg[i].b()])
                    S.dma("pool", kv_dst.t[t0:t0 + N, j * 128:(j + 1) * 128].rearrange("(a p) m -> p a m", p=c), stg[i][0:c, :, :],
                          r=[stg[i].b()], w=[kv_dst.b()])

                gemm(I["w_kv"].t, I["w_kv"].b(), 48, KT, lambda kt: hT[:, kt, 0:N], lambda kt: hT.b(kt), N, pg, evacK, wc=WC["w_kv"])
                S.barrier()
            with ExitStack() as st:
                QT = sb("bQT", [128, 24, NT], BF16, st=st)
                qm = sb("bqm", [128, c_.NQM, NT], st=st)
                ymem = sb("bymem", [128, c_.NQM, NT], BF16, st=st)
                ydil = sb("bydil", [128, 8, NT], BF16, st=st)
                with ExitStack() as st2:
                    hT = sb("bhT", [128, KT, NT], BF16, st=st2)
                    norm_phase(gmix[1], hT, st2)
                    wk = [sb("bwk%d" % i, [128, NT], st=st2) for i in range(4)]
                    pg = [ps("bpg%d" % i, [128, 512], st=st2) for i in range(3)]
                    pn = ps("bpn", [128, 512], st=st2)
                    pr = ps("bpr", [128, 512], st=st2)

                    def evacB(j, pst, M):
                        if j < 24:
                            normrope(pst, gq[:, j // 8:j // 8 + 1], QT[:, j, 0:N], QT.b(j), wk, pn, pr)
                        else:
                            V(lambda: nc.vector.tensor_copy(out=qm[:, j - 24, 0:N], in_=pst[:, 0:N]), [pst.b()], [qm.b(j - 24)])
                    gemm(I["w_in_b"].t, I["w_in_b"].b(), 24 + c_.NQM, KT, lambda kt: hT[:, kt, 0:N], lambda kt: hT.b(kt), N, pg, evacB, wc=WC["w_in_b"])
                    S.barrier()
                with ExitStack() as st2:
                    Oacc = sb("dOacc", [128, 8, NT], st=st2)
                    Dacc = sb("dDacc", [128, 8, NT], st=st2)
                    kvld = [sb("dkvld%d" % i, [128, 2048], st=st2) for i in range(2)]
                    Vbf = [sb("dVbf%d" % i, [128, 1024], BF16, st=st2) for i in range(2)]
                    KTb = [sb("dKTb%d" % i, [128, 8, 128], BF16, st=st2) for i in range(2)]
                    Pf = [sb("dPf%d" % i, [128, 128], st=st2) for i in range(2)]
                    PT = [sb("dPT%d" % i, [128, 128], BF16, st=st2) for i in range(2)]
                    ptk = [ps("dptk%d" % i, [128, 512], st=st2) for i in range(2)]
                    pS = [ps("dpS%d" % i, [128, 512], st=st2) for i in range(2)]
                    pO = [ps("dpO%d" % i, [128, 512], st=st2) for i in range(2)]
                    pD = [ps("dpD%d" % i, [128, 512], st=st2) for i in range(2)]
                    scale = 128 ** -0.5
                    units = []
                    if sample:
                        for g, (W, r) in enumerate(DIL):
                            segs = [(I["c_win%d" % g], 0, r, 128, Lm, 0, 0), (kv_dst, 0, 1, 1, Um, 0, g * 2048)]
                            units.append((g, slice(0, 1), 1, segs))
                    else:
                        for g, (W, r) in enumerate(DIL):
                            npc = N // r
                            j0 = t0 // r
                            for rho in range(r):
                                for ja in range(j0, j0 + npc, 128):
                                    nq = min(128, j0 + npc - ja)
                                    Bk, off = ja // 128, ja % 128
                                    qs = slice(rho + r * (ja - j0), rho + r * (ja - j0) + r * (nq - 1) + 1, r)
                                    segs = []
                                    if Bk >= 1:
                                        segs.append((kv_dst, rho + r * 128 * (Bk - 1), r, 128, Lm, off, g * 2048))
                                    segs.append((kv_dst, rho + r * 128 * Bk, r, off + nq, Um, off, g * 2048))
                                    units.append((g, qs, nq, segs))
                    first = [True] * 1
                    seen_g0 = set()
                    uc = [0]
                    for (g, qs, nq, segs) in units:
                        for si, (src, row0, rstep, nk, msk, moff, col0) in enumerate(segs):
                            kl = kvld[si]
                            S.dma("sp", kl[0:nk, :], src.t[row0:row0 + rstep * (nk - 1) + 1:rstep, col0:col0 + 2048], r=[src.b()], w=[kl.b()])
                            A(lambda: nc.scalar.activation(out=Vbf[si][0:nk, :], in_=kl[0:nk, 1024:2048], func=AF.Copy), [kl.b()], [Vbf[si].b()])
                            for h4 in range(0, 8, 4):
                                p = ptk[(h4 // 4) % 2]
                                for q in range(4):
                                    PE(lambda: nc.tensor.transpose(p[:, q * 128:q * 128 + nk], kl[0:nk, (h4 + q) * 128:(h4 + q + 1) * 128], ident[0:nk, 0:nk]),
                                       [kl.b()] + CB, [p.b()], inc=(q == 3))
                                V(lambda: nc.vector.tensor_copy(out=KTb[si][:, h4:h4 + 4, 0:nk],
                                                                in_=p[:].rearrange("p (q m) -> p q m", q=4)[:, :, 0:nk]),
                                  [p.b()], [KTb[si].b()])
                        for h in range(8):
                            hb, hq = h // 4, h % 4
                            pts = []
                            for si, (src, row0, rstep, nk, msk, moff, col0) in enumerate(segs):
                                i = (uc[0]) % 2
                                uc[0] += 1
                                p = pS[i]
                                PE(lambda: nc.tensor.matmul(p[0:nk, 0:nq], lhsT=KTb[si][:, h, 0:nk], rhs=QT[:, g * 8 + h, qs], start=True, stop=True),
                                   [KTb[si].b(), QT.b(g * 8 + h)], [p.b()])
                                A(lambda: nc.scalar.activation(out=Pf[i][0:nk, 0:nq], in_=p[0:nk, 0:nq], func=AF.Exp, scale=scale), [p.b()], [Pf[i].b()])
                                V(lambda: nc.vector.tensor_tensor(out=PT[i][0:nk, 0:nq], in0=Pf[i][0:nk, 0:nq], in1=msk[0:nk, moff:moff + nq], op=ALU.mult),
                                  [Pf[i].b()] + CB, [PT[i].b()])
                                pts.append((i, nk, si))
                            for k_, (i, nk, si) in enumerate(pts):
                                PE(lambda: nc.tensor.matmul(pO[hb][:, hq * 128:hq * 128 + nq], lhsT=Vbf[si][0:nk, h * 128:(h + 1) * 128], rhs=PT[i][0:nk, 0:nq],
                                                            start=(k_ == 0), stop=(k_ == len(pts) - 1)), [Vbf[si].b(), PT[i].b()], [pO[hb].b()],
                                   inc=(k_ == len(pts) - 1))
                            for k_, (i, nk, si) in enumerate(pts):
                                PE(lambda: nc.tensor.matmul(pD[hb][:, hq * 128:hq * 128 + nq], lhsT=onesb[0:nk, :], rhs=PT[i][0:nk, 0:nq],
                                                            start=(k_ == 0), stop=(k_ == len(pts) - 1)), [PT[i].b()] + CB, [pD[hb].b()],
                                   inc=(k_ == len(pts) - 1))
                        for hb in range(2):
                            for (acc_, pp_) in ((Oacc, pO[hb]), (Dacc, pD[hb])):
                                src_ = pp_[:].rearrange("p (q m) -> p q m", q=4)[:, :, 0:nq]
                                dst_ = acc_[:, hb * 4:(hb + 1) * 4, qs]
                                if g == 0:
                                    V(lambda: nc.vector.tensor_copy(out=dst_, in_=src_), [pp_.b()], [acc_.b()])
                                else:
                                    V(lambda: nc.vector.tensor_tensor(out=dst_, in0=src_, in1=dst_, op=ALU.add), [pp_.b(), acc_.b()], [acc_.b()])
                    V(lambda: nc.vector.reciprocal(out=Dacc[:, :, 0:N], in_=Dacc[:, :, 0:N]), [Dacc.b()], [Dacc.b()])
                    V(lambda: nc.vector.tensor_tensor(out=ydil[:, :, 0:N], in0=Oacc[:, :, 0:N], in1=Dacc[:, :, 0:N], op=ALU.mult),
                      [Oacc.b(), Dacc.b()], ydil.bs(range(8)))
                    S.barrier()
                with ExitStack() as st2:
                    mem_attn(1, qm, ymem, st2)
                    S.barrier()
                with ExitStack() as st2:
                    xin = [sb("bxin%d" % i, [128, NT], st=st2) for i in range(2)]
                    xo = [sb("bxo%d" % i, [128, NT], st=st2) for i in range(2)]
                    pg = [ps("bopg%d" % i, [128, 512], st=st2) for i in range(2)]
                    mv = lambda kt: (ydil[:, kt, 0:N] if kt < 8 else ymem[:, kt - 8, 0:N])
                    mvb = lambda kt: (ydil.b(kt) if kt < 8 else ymem.b(kt - 8))
                    gemm(I["w_out_b"].t, I["w_out_b"].b(), KT, c_.B_MIX // 128, mv, mvb, N, pg, make_resid_evac(N, xin, xo), wc=WC["w_out_b"])
                    S.barrier()
            ffn(1)
            with ExitStack() as st:
                XB = sb("oXB", [128, KT, NT], st=st)
                ytok = [sb("oytok%d" % i, [128, D], st=st) for i in range(2)]
                ptr = [ps("optr%d" % i, [128, 512], st=st) for i in range(2)]
                load_xres(XB, N)
                for ci in range(nch):
                    yt = ytok[ci % 2]
                    for k4 in range(0, KT, 4):
                        p = ptr[(k4 // 4) % 2]
                        for q in range(4):
                            PE(lambda: nc.tensor.transpose(p[0:c, q * 128:(q + 1) * 128], XB[:, k4 + q, ci * c:(ci + 1) * c], ident),
                               [XB.b(k4 + q)] + CB, [p.b()], inc=(q == 3))
                        V(lambda: nc.vector.tensor_copy(out=yt[0:c, k4 * 128:(k4 + 4) * 128], in_=p[0:c, :]), [p.b()], [yt.b()])
                    S.dma("pool", y_dst.t[ci * c:(ci + 1) * c, :], yt[0:c, :], r=[yt.b()], w=[y_dst.b()])
                S.barrier()

        def zero_state():
            V(lambda: nc.vector.memset(Sst[:], 0.0), [], [Sst.b()])
            V(lambda: nc.vector.memset(Sbf[:], 0.0), [], [Sbf.b()])
            V(lambda: nc.vector.memset(ctail[:], 0.0), [], ctail.bs(range(c_.NXB)))
            for i in range(2):
                V(lambda: nc.vector.memset(ftail[i][:], 0.0), [], ftail[i].bs(range(c_.NF)))

        def load_state():
            with ExitStack() as st:
                ld_ = sb("lst", [128, 128], st=st)
                rows = sb("lrows", [3, max(c_.XBC, c_.DFF)], st=st)
                pt = ps("lpt", [128, 512], st=st)
                for j in range(c_.NZ):
                    S.dma("sp", ld_[:], I["st_ssm"].t[j * 128:(j + 1) * 128, :], w=[ld_.b()])
                    PE(lambda: nc.tensor.transpose(pt[:, 0:128], ld_[:], ident), [ld_.b()] + CB, [pt.b()])
                    V(lambda: nc.vector.tensor_copy(out=Sst[:, j * 128:(j + 1) * 128], in_=pt[:, 0:128]), [pt.b()], [Sst.b()])
                V(lambda: nc.vector.tensor_copy(out=Sbf[:], in_=Sst[:]), [Sst.b()], [Sbf.b()])
                S.dma("sp", rows[0:3, 0:c_.XBC], I["st_conv"].t, w=[rows.b()])
                for j in range(c_.NXB):
                    PE(lambda: nc.tensor.transpose(pt[:, 0:3], rows[0:3, j * 128:(j + 1) * 128], ident[0:3, 0:3]), [rows.b()] + CB, [pt.b()])
                    V(lambda: nc.vector.tensor_copy(out=ctail[:, j, :], in_=pt[:, 0:3]), [pt.b()], [ctail.b(j)])
                for li in range(2):
                    S.dma("sp", rows[0:2, 0:c_.DFF], I["st_ffn"].t[li], w=[rows.b()])
                    for j in range(c_.NF):
                        PE(lambda: nc.tensor.transpose(pt[:, 0:2], rows[0:2, j * 128:(j + 1) * 128], ident[0:2, 0:2]), [rows.b()] + CB, [pt.b()])
                        V(lambda: nc.vector.tensor_copy(out=ftail[li][:, j, :], in_=pt[:, 0:2]), [pt.b()], [ftail[li].b(j)])
                S.barrier()

        def store_state(o_ssm, o_conv, o_ffn):
            with ExitStack() as st:
                stg = [sb("sst%d" % i, [128, 128], st=st) for i in range(2)]
                rows = sb("srows", [3, max(c_.XBC, c_.DFF)], st=st)
                pt = ps("sspt", [128, 512], st=st)
                for j in range(c_.NZ):
                    PE(lambda: nc.tensor.transpose(pt[:, 0:128], Sst[:, j * 128:(j + 1) * 128], ident), [Sst.b()] + CB, [pt.b()])
                    V(lambda: nc.vector.tensor_copy(out=stg[j % 2][:], in_=pt[:, 0:128]), [pt.b()], [stg[j % 2].b()])
                    S.dma("pool", o_ssm.t[j * 128:(j + 1) * 128, :], stg[j % 2][:], r=[stg[j % 2].b()], w=[o_ssm.b()])
                for j in range(c_.NXB):
                    PE(lambda: nc.tensor.transpose(pt[0:3, 0:128], ctail[:, j, :], ident), [ctail.b(j)] + CB, [pt.b()])
                    V(lambda: nc.vector.tensor_copy(out=rows[0:3, j * 128:(j + 1) * 128], in_=pt[0:3, 0:128]), [pt.b()], [rows.b()])
                S.dma("pool", o_conv.t, rows[0:3, 0:c_.XBC], r=[rows.b()], w=[o_conv.b()])
                for li in range(2):
                    for j in range(c_.NF):
                        PE(lambda: nc.tensor.transpose(pt[0:2, 0:128], ftail[li][:, j, :], ident), [ftail[li].b(j)] + CB, [pt.b()])
                        V(lambda: nc.vector.tensor_copy(out=rows[0:2, j * 128:(j + 1) * 128], in_=pt[0:2, 0:128]), [pt.b()], [rows.b()])
                    S.dma("pool", o_ffn.t[li], rows[0:2, 0:c_.DFF], r=[rows.b()], w=[o_ffn.b(li)])
                S.barrier()

        S.barrier()
        if do_prompt:
            mem_kv_prompt()
            if dbg_stop >= 2:
                zero_state()
                for ti in range(NTILES if dbg_stop >= 3 else 1):
                    wmode[0] = "fill" if ti == 0 else "use"
                    xs_ = T(I["xp"].t[ti * NT:(ti + 1) * NT, :])
                    ys_ = T(O["y_p"].t[ti * NT:(ti + 1) * NT, :])
                    trunk(ti, NT, 128, xs_, ys_, X["kvs"], ti * NT, O["memkv_p"], ti == NTILES - 1, False)
            if dbg_stop >= 4:
                store_state(O["ssm_p"], O["conv_p"], O["ffn_p"])
            if dbg_stop >= 5:
                for g in range(3):
                    L = c_.WL[g]
                    S.dma("sp", O["win%d_p" % g].t, X["kvs"].t[SEQ - L:SEQ, g * 2048:(g + 1) * 2048], r=[X["kvs"].b()], w=[O["win%d_p" % g].b()])
        if do_sample:
            wmode[0] = "use" if (do_prompt and dbg_stop >= 2) else "off"
            load_state()
            trunk(NTILES, 1, 1, I["xs"], O["y_s"], X["kvs_s"], 0, I["c_mem"], True, True)
            store_state(O["ssm_s"], O["conv_s"], O["ffn_s"])
            for g in range(3):
                S.dma("sp", O["win%d_s" % g].t, X["kvs_s"].t[0:1, g * 2048:(g + 1) * 2048], r=[X["kvs_s"].b()], w=[O["win%d_s" % g].b()])
        for E in S.ENG:
            S.flush(E)
        S.barrier()
    return nc


def prep_shared(cfg, inp):
    c_ = cfg
    KT = c_.KT
    f = lambda a: np.ascontiguousarray(np.asarray(a, np.float32))
    sh = {}
    colsA = [(j * 128, 128) for j in range(c_.NZ + c_.NXB)] + [(c_.INNER + c_.XBC, c_.HEADS)] + \
            [(c_.INNER + c_.XBC + c_.HEADS + j * 128, 128) for j in range(c_.NQM)]
    full = lambda n: [(j * 128, 128) for j in range(n)]
    sh["w_in_a"] = blockify(f(inp["w_in_a"][0]), colsA)
    sh["w_out_a"] = blockify(f(inp["w_out_a"][0]), full(KT))
    sh["w_kv"] = blockify(f(inp["w_kv"]), full(48))
    sh["w_in_b"] = blockify(f(inp["w_in_b"][0]), full(24 + c_.NQM))
    sh["w_out_b"] = blockify(f(inp["w_out_b"][0]), full(KT))
    sh["w_mem_kv"] = np.stack([blockify(f(inp["w_mem_kv"][i]), full(2 * c_.NQM)) for i in range(2)])
    upcols = []
    for j in range(c_.NF):
        upcols += [(j * 128, 128), (c_.DFF + j * 128, 128)]
    sh["w_up"] = np.stack([blockify(f(inp["w_ffn_up"][i]), upcols) for i in range(2)])
    sh["w_down"] = np.stack([blockify(f(inp["w_ffn_down"][i]), full(KT)) for i in range(2)])
    sh["g_mix"] = np.stack([pp(inp["g_mix"][i]) for i in range(2)])
    sh["g_ffn"] = np.stack([pp(inp["g_ffn"][i]) for i in range(2)])
    sh["g_kv"] = pp(inp["g_kv"])
    sh["g_mem"] = np.stack([np.tile(f(inp["g_mem"][i])[None, :], (128, 1)) for i in range(2)])
    sh["g_mem_k"] = np.stack([np.tile(f(inp["g_mem_k"][i])[None, :], (128, 1)) for i in range(2)])
    sh["g_mem_q"] = np.stack([pp(inp["g_mem_q"][i]) for i in range(2)])
    sh["g_mem_kp"] = np.stack([pp(inp["g_mem_k"][i]) for i in range(2)])
    wc = f(inp["w_conv_a"][0])
    sh["wconv"] = np.ascontiguousarray(wc.reshape(4, c_.NXB, 128).transpose(2, 1, 0))
    sh["bconv"] = pp(inp["b_conv_a"][0])
    sh["dtb"] = f(inp["dt_bias_a"][0]).reshape(-1, 1)
    sh["alog"] = f(inp["a_log_a"][0]).reshape(-1, 1)
    sh["dsk"] = pp(np.repeat(f(inp["d_skip_a"][0]), 64))
    sh["gout"] = pp(inp["g_ssm_out_a"][0])
    sh["gk"] = np.ascontiguousarray(f(inp["g_k_dil"]).T)
    sh["gq"] = np.ascontiguousarray(f(inp["g_q_dil"][0]).T)
    sh["wfc"] = np.stack([np.ascontiguousarray(f(inp["w_ffn_conv"][i]).reshape(3, c_.NF, 128).transpose(2, 1, 0)) for i in range(2)])
    sh["bfc"] = np.stack([pp(inp["b_ffn_conv"][i]) for i in range(2)])
    NTILES = c_.SEQ // c_.NT
    cosf = np.ones((NTILES + 1, 128, c_.NT), np.float32)
    sinf = np.zeros((NTILES + 1, 128, c_.NT), np.float32)
    for ti in range(NTILES):
        cosf[ti], sinf[ti] = rope_tables(np.arange(ti * c_.NT, (ti + 1) * c_.NT, dtype=np.int32))
    cs, sn = rope_tables(np.array([c_.PAST], dtype=np.int32))
    cosf[NTILES, :, 0:1], sinf[NTILES, :, 0:1] = cs, sn
    sh["cosf"], sh["sinf"] = cosf, sinf
    k = np.arange(128)
    ident = np.eye(128, dtype=np.float32)
    ones = np.ones((128, 128), np.float32)
    U = (k[:, None] <= k[None, :]).astype(np.float32)
    L = (k[:, None] >= k[None, :]).astype(np.float32)
    R = np.zeros((128, 128), np.float32)
    for i in range(16):
        R[i, i + 16] = -1.0
        R[i + 16, i] = 1.0
    sh["consts"] = np.stack([ident, ones, U, L, np.ascontiguousarray(R.T)])
    return sh


def core_inputs(cfg, inp, sh, bp, bs):
    c_ = cfg
    f = lambda a: np.ascontiguousarray(np.asarray(a, np.float32))
    m = dict(sh)
    m["xp"] = f(inp["x_prompt"][bp])
    m["xs"] = f(inp["x_sample"][bs])
    m["st_ssm"] = f(inp["state_ssm"][0, bs]).reshape(c_.INNER, 128)
    m["st_conv"] = f(inp["state_ssm_conv"][0, bs])
    m["st_ffn"] = f(inp["state_ffn_conv"][:, bs])
    m["c_mem"] = f(inp["cache_mem_kv"][:, bs]).reshape(2, 256, 2 * c_.MEMW)
    cw = (inp["cache_win_kv0"], inp["cache_win_kv1"], inp["cache_win_kv2"])
    for g in range(3):
        m["c_win%d" % g] = f(cw[g][bs]).reshape(c_.CL[g], 2048)
    m["memp"] = f(inp["mem_prompt"][bp])
    return m


def assemble(cfg, res, nbp, nbs):
    c_ = cfg
    R = res
    st = lambda key, n: np.stack([R[i][key] for i in range(n)])
    y_p = st("y_p", nbp)
    y_s = st("y_s", nbs)
    ssm_p = st("ssm_p", nbp).reshape(1, nbp, c_.HEADS, 64, 128)
    ssm_s = st("ssm_s", nbs).reshape(1, nbs, c_.HEADS, 64, 128)
    conv_p = st("conv_p", nbp)[None]
    conv_s = st("conv_s", nbs)[None]
    ffn_p = np.ascontiguousarray(st("ffn_p", nbp).transpose(1, 0, 2, 3))
    ffn_s = np.ascontiguousarray(st("ffn_s", nbs).transpose(1, 0, 2, 3))
    memkv = np.ascontiguousarray(st("memkv_p", nbp).transpose(1, 0, 2, 3)).reshape(2, nbp, 256, 2, 4, c_.MHD)
    outs = [y_p, y_s, ssm_p, ssm_s, conv_p, conv_s, ffn_p, ffn_s, memkv]
    for g in range(3):
        outs.append(st("win%d_p" % g, nbp).reshape(nbp, c_.WL[g], 2, 8, 128))
    for g in range(3):
        outs.append(st("win%d_s" % g, nbs).reshape(nbs, 1, 2, 8, 128))
    return tuple(np.ascontiguousarray(o.astype(np.float32)) for o in outs)


def kernel(**inputs):
    cfg = Cfg()
    nc = build(cfg)
    sh = prep_shared(cfg, inputs)
    in_maps = [core_inputs(cfg, inputs, sh, i % 4, i) for i in range(8)]
    res = run_bass_kernel_spmd(nc, in_maps, core_ids=list(range(8)))
    return assemble(cfg, res.results, 4, 8)
```
